# Optimizing a Trainium2 kernel written in Bass

```python
import jax, jax.numpy as jnp
from jax import lax
import numpy as np

D_MODEL = 1024
BATCH = 8
SEQ = 4096
DEPTH = 4

HEAD_DIM = 64
D_MIX = D_MODEL
D_FF = 2752
EPS = 1e-6
NEG_INF = -1e30
BLOCK = 128
CONV_CH = D_MODEL // 2
CONV_WIDTH = 31
SWA_Q_HEADS = (D_MIX - CONV_CH) // HEAD_DIM
SWA_KV_HEADS = 2
SWA_GROUP = SWA_Q_HEADS // SWA_KV_HEADS
SWA_WINDOW = 128
EVEN_IN = 2 * CONV_CH + (SWA_Q_HEADS + 2 * SWA_KV_HEADS) * HEAD_DIM
DIL_HEADS = 8
DIL_PAIRS = ((128, 1), (512, 4), (2048, 16))
POOL_CH = D_MIX - DIL_HEADS * HEAD_DIM
POOL_SIZES = (2, 4, 8, 16)
POOL_GROUP = POOL_CH // len(POOL_SIZES)
ODD_IN = 3 * DIL_HEADS * HEAD_DIM + POOL_CH
N_EVEN = (DEPTH + 1) // 2
N_ODD = DEPTH // 2

kernel_name = "hybrid_conv_swa_dilated_pool_macaron"


def rms_norm(x, g):
    xf = x.astype(jnp.float32)
    y = xf * lax.rsqrt(jnp.mean(xf * xf, axis=-1, keepdims=True) + EPS)
    return (y * g.astype(jnp.float32)).astype(x.dtype)


def layer_norm(x, g, b):
    xf = x.astype(jnp.float32)
    mu = jnp.mean(xf, axis=-1, keepdims=True)
    var = jnp.mean(jnp.square(xf - mu), axis=-1, keepdims=True)
    y = (xf - mu) * lax.rsqrt(var + EPS)
    return (y * g.astype(jnp.float32) + b.astype(jnp.float32)).astype(x.dtype)


def swiglu(x, w_gate, w_up, w_down):
    return (jax.nn.silu(x @ w_gate) * (x @ w_up)) @ w_down


def banded_attention(q, k, v, max_dist, sink=None):
    b, L, kvh, grp, dh = q.shape
    n = -(-L // BLOCK)
    pad = n * BLOCK - L
    qb = jnp.pad(q, ((0, 0), (0, pad), (0, 0), (0, 0), (0, 0))).reshape(b, n, BLOCK, kvh, grp, dh)

    def windows(t):
        tb = jnp.pad(t, ((0, 0), (BLOCK, pad), (0, 0), (0, 0))).reshape(b, n + 1, BLOCK, kvh, dh)
        return jnp.concatenate([tb[:, :-1], tb[:, 1:]], axis=2)

    kw, vw = windows(k), windows(v)
    s = jnp.einsum('bnqhgd,bnkhd->bnhgqk', qb, kw).astype(jnp.float32) * (dh ** -0.5)
    qi = np.arange(BLOCK)[:, None]
    kj = np.arange(2 * BLOCK)[None, :]
    blk = np.arange(n)[:, None, None]
    dist = qi + BLOCK - kj
    valid = (dist >= 0) & (dist <= max_dist) & (blk * BLOCK - BLOCK + kj >= 0)
    s = jnp.where(valid[None, :, None, None], s, NEG_INF)
    m = jnp.max(s, axis=-1)
    if sink is not None:
        m = jnp.maximum(m, sink[None, None, :, :, None])
    p = jnp.exp(s - m[..., None])
    l = jnp.sum(p, axis=-1)
    if sink is not None:
        l = l + jnp.exp(sink[None, None, :, :, None] - m)
    o = jnp.einsum('bnhgqk,bnkhd->bnqhgd', p.astype(v.dtype), vw).astype(jnp.float32)
    m = jnp.moveaxis(m, -1, 2)
    l = jnp.moveaxis(l, -1, 2)
    o = o / l[..., None]

    def to_seq(t):
        return t.reshape(b, n * BLOCK, *t.shape[3:])[:, :L]

    return to_seq(o).astype(q.dtype), to_seq(m), to_seq(l)


def even_mixer(h, w_in, w_out, conv_w, conv_b, ln_g, ln_b, q_g, k_g, sinks):
    b, s, _ = h.shape
    proj = h @ w_in
    cuts = [CONV_CH, 2 * CONV_CH, 2 * CONV_CH + SWA_Q_HEADS * HEAD_DIM,
            2 * CONV_CH + (SWA_Q_HEADS + SWA_KV_HEADS) * HEAD_DIM]
    a_val, a_gate, q, k, v = jnp.split(proj, cuts, axis=-1)
    u = a_val * jax.nn.sigmoid(a_gate)
    u = lax.conv_general_dilated(u, conv_w[:, None, :].astype(u.dtype), (1,), ((CONV_WIDTH - 1, 0),),
                                 dimension_numbers=('NWC', 'WIO', 'NWC'),
                                 feature_group_count=CONV_CH) + conv_b
    u = jax.nn.silu(layer_norm(u, ln_g, ln_b))
    q = rms_norm(q.reshape(b, s, SWA_KV_HEADS, SWA_GROUP, HEAD_DIM), q_g)
    k = rms_norm(k.reshape(b, s, SWA_KV_HEADS, HEAD_DIM), k_g)
    v = v.reshape(b, s, SWA_KV_HEADS, HEAD_DIM)
    o, _, _ = banded_attention(q, k, v, SWA_WINDOW - 1,
                               sinks.reshape(SWA_KV_HEADS, SWA_GROUP).astype(jnp.float32))
    return jnp.concatenate([u, o.reshape(b, s, -1)], axis=-1) @ w_out


def dilated_branch(q, k, v, window, dilation):
    b, s, h, dh = q.shape
    L = s // dilation

    def to_res(t):
        return t.reshape(b, L, dilation, h, dh).transpose(0, 2, 1, 3, 4).reshape(b * dilation, L, h, dh)

    def from_res(t):
        rest = t.shape[2:]
        return t.reshape(b, dilation, L, *rest).swapaxes(1, 2).reshape(b, s, *rest)

    o, m, l = banded_attention(to_res(q)[:, :, :, None], to_res(k), to_res(v), window // dilation)
    return from_res(o[:, :, :, 0]), from_res(m[..., 0]), from_res(l[..., 0])


def multiscale_pool(u, pool_w, pool_scale):
    b, s, _ = u.shape
    uf = u.astype(jnp.float32)
    cs = jnp.pad(jnp.cumsum(uf, axis=1), ((0, 0), (1, 0), (0, 0)))
    outs = []
    for gi, w in enumerate(POOL_SIZES):
        sl = slice(gi * POOL_GROUP, (gi + 1) * POOL_GROUP)
        c = cs[:, :, sl]
        lagged = jnp.pad(c[:, :s + 1 - w], ((0, 0), (w - 1, 0), (0, 0)))
        count = jnp.minimum(jnp.arange(1, s + 1), w).astype(jnp.float32)[:, None]
        pooled = ((c[:, 1:] - lagged) / count - uf[:, :, sl]).astype(u.dtype)
        outs.append(pooled @ pool_w[gi])
    return jnp.concatenate(outs, axis=-1) * pool_scale


def odd_mixer(h, w_in, w_out, q_g, k_g, pool_w, pool_scale):
    b, s, _ = h.shape
    hd = DIL_HEADS * HEAD_DIM
    q, k, v, u = jnp.split(h @ w_in, [hd, 2 * hd, 3 * hd], axis=-1)
    q = rms_norm(q.reshape(b, s, DIL_HEADS, HEAD_DIM), q_g)
    k = rms_norm(k.reshape(b, s, DIL_HEADS, HEAD_DIM), k_g)
    v = v.reshape(b, s, DIL_HEADS, HEAD_DIM)
    outs, ms, ls = zip(*[dilated_branch(q, k, v, w, d) for w, d in DIL_PAIRS])
    ms = jnp.stack(ms)
    wts = jnp.stack(ls) * jnp.exp(ms - jnp.max(ms, axis=0))
    att = jnp.einsum('rbsh,rbshd->bshd', wts, jnp.stack(outs).astype(jnp.float32))
    att = (att / jnp.sum(wts, axis=0)[..., None]).astype(h.dtype)
    pool = multiscale_pool(u, pool_w, pool_scale)
    return jnp.concatenate([att.reshape(b, s, hd), pool], axis=-1) @ w_out


def setup_inputs(seed: int = 0) -> dict:
    key = jax.random.key(seed)
    ks = jax.random.split(key, 24)

    def nrm(k, shape, scale):
        return scale * jax.random.normal(k, shape, jnp.float32)

    return {
        "x": nrm(ks[0], (BATCH, SEQ, D_MODEL), 1.0),
        "norm_g": 1.0 + nrm(ks[1], (DEPTH, 3, D_MODEL), 0.05),
        "ffn_w_gate": nrm(ks[2], (DEPTH, 2, D_MODEL, D_FF), D_MODEL ** -0.5),
        "ffn_w_up": nrm(ks[3], (DEPTH, 2, D_MODEL, D_FF), D_MODEL ** -0.5),
        "ffn_w_down": nrm(ks[4], (DEPTH, 2, D_FF, D_MODEL), D_FF ** -0.5),
        "ev_w_in": nrm(ks[5], (N_EVEN, D_MODEL, EVEN_IN), D_MODEL ** -0.5),
        "ev_w_out": nrm(ks[6], (N_EVEN, D_MIX, D_MODEL), D_MIX ** -0.5),
        "ev_conv_w": nrm(ks[7], (N_EVEN, CONV_WIDTH, CONV_CH), CONV_WIDTH ** -0.5),
        "ev_conv_b": nrm(ks[8], (N_EVEN, CONV_CH), 0.02),
        "ev_ln_g": 1.0 + nrm(ks[9], (N_EVEN, CONV_CH), 0.05),
        "ev_ln_b": nrm(ks[10], (N_EVEN, CONV_CH), 0.02),
        "ev_q_norm_g": 1.0 + nrm(ks[11], (N_EVEN, HEAD_DIM), 0.05),
        "ev_k_norm_g": 1.0 + nrm(ks[12], (N_EVEN, HEAD_DIM), 0.05),
        "ev_sinks": nrm(ks[13], (N_EVEN, SWA_Q_HEADS), 0.5),
        "od_w_in": nrm(ks[14], (N_ODD, D_MODEL, ODD_IN), D_MODEL ** -0.5),
        "od_w_out": nrm(ks[15], (N_ODD, D_MIX, D_MODEL), D_MIX ** -0.5),
        "od_q_norm_g": 1.0 + nrm(ks[16], (N_ODD, HEAD_DIM), 0.05),
        "od_k_norm_g": 1.0 + nrm(ks[17], (N_ODD, HEAD_DIM), 0.05),
        "od_pool_w": nrm(ks[18], (N_ODD, len(POOL_SIZES), POOL_GROUP, POOL_GROUP), POOL_GROUP ** -0.5),
        "od_pool_scale": 1.0 + nrm(ks[19], (N_ODD, POOL_CH), 0.05),
    }


def reference(x, norm_g, ffn_w_gate, ffn_w_up, ffn_w_down,
              ev_w_in, ev_w_out, ev_conv_w, ev_conv_b, ev_ln_g, ev_ln_b,
              ev_q_norm_g, ev_k_norm_g, ev_sinks,
              od_w_in, od_w_out, od_q_norm_g, od_k_norm_g, od_pool_w, od_pool_scale):
    for layer in range(DEPTH):
        g = norm_g[layer]
        x = x + 0.5 * swiglu(rms_norm(x, g[0]), ffn_w_gate[layer, 0], ffn_w_up[layer, 0], ffn_w_down[layer, 0])
        h = rms_norm(x, g[1])
        i = layer // 2
        if layer % 2 == 0:
            mix = even_mixer(h, ev_w_in[i], ev_w_out[i], ev_conv_w[i], ev_conv_b[i], ev_ln_g[i], ev_ln_b[i],
                             ev_q_norm_g[i], ev_k_norm_g[i], ev_sinks[i])
        else:
            mix = odd_mixer(h, od_w_in[i], od_w_out[i], od_q_norm_g[i], od_k_norm_g[i],
                            od_pool_w[i], od_pool_scale[i])
        x = x + mix
        x = x + 0.5 * swiglu(rms_norm(x, g[2]), ffn_w_gate[layer, 1], ffn_w_up[layer, 1], ffn_w_down[layer, 1])
    return x
```

```python
import numpy as np
from contextlib import ExitStack
import concourse.bass as bass
import concourse.mybir as mybir
from concourse.bass_utils import run_bass_kernel_spmd

F32 = mybir.dt.float32
BF16 = mybir.dt.bfloat16
AF = mybir.ActivationFunctionType
ALU = mybir.AluOpType

D = 1024
S = 4096
NB = 8
DEPTH = 4
DFF = 2752
DFFP = 2816
NM = 22
TT = 512
NT = S // TT
EPS = 1e-6
CONVW = 31
NEG = -30000.0
POOL_SIZES = (2, 4, 8, 16)
DIL = ((128, 1), (512, 4), (2048, 16))

PC_G = 0
PC_EV = 96
EV_CW, EV_CB, EV_LG, EV_LB, EV_QG, EV_KG, EV_SK = 0, 124, 128, 132, 136, 137, 138
EV_N = 146
PC_OD = PC_EV + 2 * EV_N
OD_QG, OD_KG, OD_PS = 0, 1, 2
OD_N = 6
PC_N = PC_OD + 2 * OD_N
C_ID, C_MPS, C_MD, C_MPD, C_MD2, C_OB = 0, 128, 256, 384, 512, 640
CBF = 768
C_IC = 768
CN = 832
EV_UNITS = 22
OD_UNITS = 25


def mix_base(layer):
    i = layer // 2
    return i * (EV_UNITS + OD_UNITS) + (0 if layer % 2 == 0 else EV_UNITS)


N_UNITS = 2 * (EV_UNITS + OD_UNITS)


class Tok:
    __slots__ = ("w", "r", "multi")

    def __init__(self, multi=False):
        self.w = {}
        self.r = {}
        self.multi = multi


class Sem:
    def __init__(self, h, dma=False):
        self.h = h
        self.n = 0
        self.dma = dma


class KB:
    def __init__(self, nc):
        self.nc = nc
        self.E = dict(pe=nc.tensor, act=nc.scalar, dve=nc.vector, pool=nc.gpsimd, sp=nc.sync)
        self.prog = {e: Sem(nc.alloc_semaphore("prog_" + e)) for e in self.E}
        self.seen = {e: {} for e in self.E}
        self.nwait = 0
        self.nins = 0
        self._dsems = []

    def dsem(self, name):
        s = Sem(self.nc.alloc_semaphore("d_" + name), dma=True)
        self._dsems.append(s)
        return s

    def barrier(self):
        sems = list(self.prog.values()) + self._dsems
        for eng, h in self.E.items():
            seen = self.seen[eng]
            for s in sems:
                if s is self.prog[eng] or s.n == 0:
                    continue
                if seen.get(s, 0) < s.n:
                    h.wait_ge(s.h, s.n)
                    seen[s] = s.n
                    self.nwait += 1

    def op(self, eng, fn, rd=(), wr=(), dsem=None, inc=True):
        deps = {}

        def add(s, v):
            if s.dma:
                v = s.n
            if deps.get(s, 0) < v:
                deps[s] = v

        for t in rd:
            for s, v in t.w.items():
                add(s, v)
        for t in wr:
            if not t.multi:
                for s, v in t.w.items():
                    add(s, v)
            for s, v in t.r.items():
                add(s, v)
        seen = self.seen[eng]
        h = self.E[eng]
        mine = self.prog[eng]
        for s, v in deps.items():
            if eng == "pe" and s is mine:
                continue
            if seen.get(s, 0) < v:
                assert v < 65000, "semaphore count too large"
                h.wait_ge(s.h, v)
                seen[s] = v
                self.nwait += 1
        ins = fn(h)
        self.nins += 1
        if dsem is not None:
            dsem.n += 16
            ins.then_inc(dsem.h, 16)
            mark = (dsem, dsem.n)
        else:
            if inc:
                mine.n += 1
                ins.then_inc(mine.h, 1)
                mark = (mine, mine.n)
            else:
                mark = (mine, mine.n + 1)
        for t in rd:
            if t.r.get(mark[0], 0) < mark[1]:
                t.r[mark[0]] = mark[1]
        for t in wr:
            if t.multi:
                t.w[mark[0]] = mark[1]
            else:
                t.w = {mark[0]: mark[1]}
                t.r = {}
        return ins


def cols(buf, rows, start, count, step):
    if step == 1:
        return buf[rows, start:start + count]
    return buf[rows, start:start + step * (count - 1) + 1:step]


def cols3(buf, rows, start, count, step):
    if step == 1:
        return buf[rows, :, start:start + count]
    return buf[rows, :, start:start + step * (count - 1) + 1:step]


def build_program(stop_after=None, skip=()):
    nc = bass.Bass("TRN2", target_bir_lowering=False)
    kb = KB(nc)
    op = kb.op
    uid = [0]

    def sbt(name, shape, dt):
        uid[0] += 1
        return nc.sbuf_tensor("%s_%d" % (name, uid[0]), shape, dt)

    x_in = nc.dram_tensor("xT", [D, S], F32, kind="ExternalInput").ap()
    y_out = nc.dram_tensor("yT", [D, S], F32, kind="ExternalOutput").ap()
    pcols_in = nc.dram_tensor("pcols", [128, PC_N], F32, kind="ExternalInput").ap()
    consts_in = nc.dram_tensor("consts", [128, CN], F32, kind="ExternalInput").ap()
    wgu_in = nc.dram_tensor("wgu", [DEPTH * 2 * NM, 128, 2048], F32, kind="ExternalInput").ap()
    wd_in = nc.dram_tensor("wd", [DEPTH * 2 * 8, 128, NM * 128], F32, kind="ExternalInput").ap()
    wmix_in = nc.dram_tensor("wmix", [N_UNITS, 128, 1024], F32, kind="ExternalInput").ap()
    wgu_b = nc.dram_tensor("wgu_b", [DEPTH * 2 * NM, 128, 2048], BF16).ap()
    wd_b = nc.dram_tensor("wd_b", [DEPTH * 2 * 8, 128, NM * 128], BF16).ap()
    wmix_b = nc.dram_tensor("wmix_b", [N_UNITS, 128, 1024], BF16).ap()
    qT_d = nc.dram_tensor("qT_d", [512, S], BF16).ap()
    kT_d = nc.dram_tensor("kT_d", [512, S], BF16).ap()
    v_d = nc.dram_tensor("v_d", [S, 512], BF16).ap()
    u_d = nc.dram_tensor("u_d", [512, S], BF16).ap()
    mixT_d = nc.dram_tensor("mixT_d", [D, S], BF16).ap()
    t_qd, t_kd, t_vd, t_ud, t_mixd = Tok(True), Tok(True), Tok(True), Tok(True), Tok(True)

    xs = nc.alloc_sbuf_tensor("xs", [128, 8, S], F32)
    pcols = nc.alloc_sbuf_tensor("pcols_sb", [128, PC_N], F32)
    icnt = nc.alloc_sbuf_tensor("icnt", [128, 64], F32)
    cbf = nc.alloc_sbuf_tensor("cbf", [128, CBF], BF16)
    ones_b = nc.alloc_sbuf_tensor("ones_b", [128, 128], BF16)
    ident = cbf[:, C_ID:C_ID + 128]
    ones_blk = cbf[:, C_OB:C_OB + 128]
    tx = [[Tok() for t in range(NT)] for c in range(8)]
    t_const = Tok()
    ps = [nc.alloc_psum_tensor("ps%d" % i, [128, 512], F32) for i in range(8)]
    tps = [Tok() for i in range(8)]

    ds_x = kb.dsem("x")
    ds_c = kb.dsem("c")
    for c in range(8):
        for t in range(NT):
            op("sp", lambda h: h.dma_start(out=xs[:, c, t * TT:(t + 1) * TT],
                                           in_=x_in[c * 128:(c + 1) * 128, t * TT:(t + 1) * TT]),
               wr=[tx[c][t]], dsem=ds_x)
    with sbt("c32", [128, CBF], F32) as c32:
        op("sp", lambda h: h.dma_start(out=pcols[:], in_=pcols_in[:, :]), wr=[t_const], dsem=ds_c)
        op("sp", lambda h: h.dma_start(out=c32[:], in_=consts_in[:, 0:CBF]), wr=[t_const], dsem=ds_c)
        op("sp", lambda h: h.dma_start(out=icnt[:], in_=consts_in[:, C_IC:C_IC + 64]), wr=[t_const], dsem=ds_c)
        op("dve", lambda h: h.memset(ones_b[:], 1.0), wr=[t_const])
        op("dve", lambda h: h.tensor_copy(out=cbf[:], in_=c32[:]), rd=[t_const], wr=[t_const])
        kb.barrier()

    t_wgub = [Tok(True) for f in range(DEPTH * 2)]
    t_wgub0 = [Tok(True) for m in range(4)]
    t_wdb = [Tok(True) for f in range(DEPTH * 2)]
    t_wmixb = [Tok(True) for l in range(DEPTH)]
    conv_q = []
    conv_sems = {}

    def conv_add(key, src, dst, tdst):
        if key not in conv_sems:
            conv_sems[key] = kb.dsem("cv_%s" % str(key))
        sem = conv_sems[key]
        conv_q.append((key, lambda: op("pool", lambda h: h.dma_start(out=dst, in_=src), wr=[tdst], dsem=sem)))

    def conv_plan(phases):
        for layer, p in phases:
            if p == "mix":
                b0 = mix_base(layer)
                nu = EV_UNITS if layer % 2 == 0 else OD_UNITS
                u = 0
                while u < nu:
                    k = min(2, nu - u)
                    conv_add(("m", layer), wmix_in[b0 + u:b0 + u + k].rearrange("u p n -> p u n"),
                             wmix_b[b0 + u:b0 + u + k].rearrange("u p n -> p u n"), t_wmixb[layer])
                    u += k
            else:
                f = layer * 2 + (0 if p == "ffn1" else 1)
                for m in range(NM):
                    if f == 0 and m < 4:
                        conv_add(("f0", m), wgu_in[m], wgu_b[m], t_wgub0[m])
                    else:
                        conv_add(("f", f), wgu_in[f * NM + m], wgu_b[f * NM + m], t_wgub[f])
                H = NM * 64
                for c in range(8):
                    for hh in range(2):
                        conv_add(("f", f), wd_in[f * 8 + c][:, hh * H:(hh + 1) * H],
                                 wd_b[f * 8 + c][:, hh * H:(hh + 1) * H], t_wdb[f])

    def bgconv(n=1):
        for _ in range(n):
            if conv_q:
                conv_q.pop(0)[1]()

    def conv_need(key):
        last = -1
        for i, (k, _) in enumerate(conv_q):
            if k == key:
                last = i
        for _ in range(last + 1):
            conv_q.pop(0)[1]()

    def rms_tile(tt, gbase, xn_ap, t_xn, sq, t_sq, rstd, t_rstd, pn=0, sq_eng="pool"):
        sl = slice(tt * TT, (tt + 1) * TT)
        for k in range(8):
            j = k % len(sq)
            if sq_eng == "pool":
                op("pool", lambda h: h.tensor_tensor(out=sq[j][:], in0=xs[:, k, sl], in1=xs[:, k, sl], op=ALU.mult),
                   rd=[tx[k][tt]], wr=[t_sq[j]])
            else:
                op("act", lambda h: h.activation(out=sq[j][:], in_=xs[:, k, sl], func=AF.Square),
                   rd=[tx[k][tt]], wr=[t_sq[j]])
            op("pe", lambda h: h.matmul(ps[pn][:], ones_b[:], sq[j][:], start=(k == 0), stop=(k == 7)),
               rd=[t_sq[j], t_const], wr=[tps[pn]])
        op("act", lambda h: h.activation(out=rstd[:], in_=ps[pn][:], func=AF.Ln, scale=1.0 / D, bias=EPS),
           rd=[tps[pn]], wr=[t_rstd])
        op("act", lambda h: h.activation(out=rstd[:], in_=rstd[:], func=AF.Exp, scale=-0.5), rd=[t_rstd], wr=[t_rstd])
        for k in range(8):
            op("dve", lambda h: h.scalar_tensor_tensor(out=xn_ap[:, k, :], in0=xs[:, k, sl],
                                                       scalar=pcols[:, gbase + k:gbase + k + 1],
                                                       in1=rstd[:], op0=ALU.mult, op1=ALU.mult),
               rd=[tx[k][tt], t_rstd, t_const], wr=[t_xn])

    def rms_stage(tt, gbase, xn_ap, t_xn, sq, t_sq, rstd, t_rstd, stage, pn=0):
        sl = slice(tt * TT, (tt + 1) * TT)
        if stage < 8:
            k = stage
            j = k % len(sq)
            op("act", lambda h: h.activation(out=sq[j][:], in_=xs[:, k, sl], func=AF.Square),
               rd=[tx[k][tt]], wr=[t_sq[j]])
            op("pe", lambda h: h.matmul(ps[pn][:], ones_b[:], sq[j][:], start=(k == 0), stop=(k == 7)),
               rd=[t_sq[j], t_const], wr=[tps[pn]])
        elif stage == 8:
            op("act", lambda h: h.activation(out=rstd[:], in_=ps[pn][:], func=AF.Ln, scale=1.0 / D, bias=EPS),
               rd=[tps[pn]], wr=[t_rstd])
        elif stage == 9:
            op("act", lambda h: h.activation(out=rstd[:], in_=rstd[:], func=AF.Exp, scale=-0.5),
               rd=[t_rstd], wr=[t_rstd])
        else:
            k = stage - 10
            op("dve", lambda h: h.scalar_tensor_tensor(out=xn_ap[:, k, :], in0=xs[:, k, sl],
                                                       scalar=pcols[:, gbase + k:gbase + k + 1],
                                                       in1=rstd[:], op0=ALU.mult, op1=ALU.mult),
               rd=[tx[k][tt], t_rstd, t_const], wr=[t_xn])

    class NormBufs:
        def __init__(self, es, nsq=3, nxn=2):
            self.xn = [es.enter_context(sbt("xn%d" % i, [128, 8, TT], BF16)) for i in range(nxn)]
            self.t_xn = [Tok() for _ in range(nxn)]
            self.sq = [es.enter_context(sbt("sq%d" % i, [128, TT], BF16)) for i in range(nsq)]
            self.t_sq = [Tok() for _ in range(nsq)]
            self.rstd = es.enter_context(sbt("rstd", [128, TT], F32))
            self.t_rstd = Tok()

        def norm(self, tt, gbase, b, sq_eng="pool"):
            rms_tile(tt, gbase, self.xn[b], self.t_xn[b], self.sq, self.t_sq, self.rstd, self.t_rstd, sq_eng=sq_eng)

        def norm_part(self, tt, gbase, b, part):
            sl = slice(tt * TT, (tt + 1) * TT)
            xn_ap, t_xn, sq, t_sq, rstd, t_rstd = self.xn[b], self.t_xn[b], self.sq, self.t_sq, self.rstd, self.t_rstd
            if part == 0:
                for k in range(8):
                    op("pool", lambda h: h.tensor_tensor(out=sq[k][:], in0=xs[:, k, sl], in1=xs[:, k, sl], op=ALU.mult),
                       rd=[tx[k][tt]], wr=[t_sq[k]])
            elif part == 1:
                for k in range(8):
                    op("pe", lambda h: h.matmul(ps[0][:], ones_b[:], sq[k][:], start=(k == 0), stop=(k == 7)),
                       rd=[t_sq[k], t_const], wr=[tps[0]], inc=(k == 7))
                op("act", lambda h: h.activation(out=rstd[:], in_=ps[0][:], func=AF.Ln, scale=1.0 / D, bias=EPS),
                   rd=[tps[0]], wr=[t_rstd])
                op("act", lambda h: h.activation(out=rstd[:], in_=rstd[:], func=AF.Exp, scale=-0.5),
                   rd=[t_rstd], wr=[t_rstd])
            else:
                for k in range((part - 2) * 4, (part - 2) * 4 + 4):
                    op("dve", lambda h: h.scalar_tensor_tensor(out=xn_ap[:, k, :], in0=xs[:, k, sl],
                                                               scalar=pcols[:, gbase + k:gbase + k + 1],
                                                               in1=rstd[:], op0=ALU.mult, op1=ALU.mult),
                       rd=[tx[k][tt], t_rstd, t_const], wr=[t_xn])

        def norm_stage(self, tt, gbase, b, stage):
            rms_stage(tt, gbase, self.xn[b], self.t_xn[b], self.sq, self.t_sq, self.rstd, self.t_rstd, stage)

    shared = {}

    def get_sems(key, names):
        if key not in shared:
            shared[key] = [kb.dsem(n) for n in names]
        return shared[key]

    def proj8(bank, w_ap_fn, xn, t_w, t_xn):
        for k in range(8):
            op("pe", lambda h: h.matmul(ps[bank][:], w_ap_fn(k), xn[:, k, :], start=(k == 0), stop=(k == 7)),
               rd=[t_w, t_xn], wr=[tps[bank]], inc=(k == 7))

    def ffn(f, layer, which):
        gbase = PC_G + (layer * 3 + (0 if which == 0 else 2)) * 8
        conv_need(("f", f))
        with ExitStack() as es:
            sb = lambda name, shape, dt: es.enter_context(sbt(name, shape, dt))
            nb = NormBufs(es)
            xn, t_xn = nb.xn, nb.t_xn
            act = sb("act", [128, NM, TT], BF16)
            t_act = [Tok() for _ in range(NM)]
            wgu = [sb("wgu%d" % i, [128, 2, 8, 128], BF16) for i in range(3)]
            t_wgu = [Tok() for _ in range(3)]
            d_wgu = get_sems("wgu", ["wgu0", "wgu1", "wgu2"])
            d_wd = get_sems("wd", ["wd0", "wd1"])
            wd = [sb("wd%d" % i, [128, NM, 128], BF16) for i in range(2)]
            t_wd = [Tok() for _ in range(2)]
            sg = [sb("sg%d" % i, [128, TT], BF16) for i in range(2)]
            t_sg = [Tok() for _ in range(2)]
            pg, pu, py = (1, 2), (3, 4), (5, 6)
            gi = [0]
            di = [0]

            def load_gu(m):
                s = gi[0] % 3
                gi[0] += 1
                tsrc = t_wgub0[m] if (f == 0 and m < 4) else t_wgub[f]
                op("sp", lambda h: h.dma_start(out=wgu[s][:], in_=wgu_b[f * NM + m]),
                   rd=[tsrc], wr=[t_wgu[s]], dsem=d_wgu[s])
                return s

            def load_d(c):
                s = di[0] % 2
                di[0] += 1
                op("sp", lambda h: h.dma_start(out=wd[s][:], in_=wd_b[f * 8 + c]),
                   rd=[t_wdb[f]], wr=[t_wd[s]], dsem=d_wd[s])
                return s

            nb.norm(0, gbase, 0, sq_eng="act")
            for tt in range(NT):
                b = tt % 2
                sl = slice(tt * TT, (tt + 1) * TT)
                slots = {0: load_gu(0)}
                slots[1] = load_gu(1)
                dslots = {}
                for m in range(NM):
                    if m + 2 < NM:
                        slots[m + 2] = load_gu(m + 2)
                    if m == NM - 2:
                        dslots[0] = load_d(0)
                    s = slots[m]
                    par = m % 2
                    for g, bank in ((0, pg[par]), (1, pu[par])):
                        for k in range(8):
                            op("pe", lambda h: h.matmul(ps[bank][:], wgu[s][:, g, k, :], xn[b][:, k, :],
                                                        start=(k == 0), stop=(k == 7)),
                               rd=[t_wgu[s], t_xn[b]], wr=[tps[bank]], inc=(k == 7))
                    op("act", lambda h: h.activation(out=sg[par][:], in_=ps[pg[par]][:], func=AF.Silu),
                       rd=[tps[pg[par]]], wr=[t_sg[par]])
                    op("dve", lambda h: h.tensor_tensor(out=act[:, m, :], in0=sg[par][:], in1=ps[pu[par]][:],
                                                        op=ALU.mult),
                       rd=[t_sg[par], tps[pu[par]]], wr=[t_act[m]])
                    if tt + 1 < NT:
                        if 1 <= m <= 8:
                            nb.norm_stage(tt + 1, gbase, 1 - b, m - 1)
                        elif m == 9:
                            nb.norm_stage(tt + 1, gbase, 1 - b, 8)
                            nb.norm_stage(tt + 1, gbase, 1 - b, 9)
                        elif 10 <= m <= 17:
                            nb.norm_stage(tt + 1, gbase, 1 - b, m)
                    if m % 8 == 1:
                        bgconv(1)
                for c in range(8):
                    if c + 1 < 8:
                        dslots[c + 1] = load_d(c + 1)
                    s = dslots[c]
                    bank = py[c % 2]
                    for k in range(NM):
                        op("pe", lambda h: h.matmul(ps[bank][:], wd[s][:, k, :], act[:, k, :],
                                                    start=(k == 0), stop=(k == NM - 1)),
                           rd=[t_wd[s], t_act[k]], wr=[tps[bank]], inc=(k == NM - 1))
                    op("dve", lambda h: h.scalar_tensor_tensor(out=xs[:, c, sl], in0=ps[bank][:], scalar=0.5,
                                                               in1=xs[:, c, sl], op0=ALU.mult, op1=ALU.add),
                       rd=[tps[bank], tx[c][tt]], wr=[tx[c][tt]])
            kb.barrier()

    hn_i = [0]
    pr_i = [0]

    def head_norm_a(bank, tmp):
        sqh_l, t_sqh_l, rq_l, t_rq_l, ssb = tmp
        j = hn_i[0] % len(sqh_l)
        hn_i[0] += 1
        sqh, t_sqh = sqh_l[j], t_sqh_l[j]
        op("act", lambda h: h.activation(out=sqh[:], in_=ps[bank][:], func=AF.Square),
           rd=[tps[bank]], wr=[t_sqh])
        op("pe", lambda h: h.matmul(ps[ssb][:], ones_blk, sqh[:], start=True, stop=True),
           rd=[t_sqh, t_const], wr=[tps[ssb]])
        return (bank, ssb, rq_l[j], t_rq_l[j])

    def head_norm_b(st, gcol, dst_ap, t_dst):
        bank, ssb, rq, t_rq = st
        op("act", lambda h: h.activation(out=rq[:], in_=ps[ssb][:], func=AF.Ln, scale=1.0 / 64, bias=EPS),
           rd=[tps[ssb]], wr=[t_rq])
        op("act", lambda h: h.activation(out=rq[:], in_=rq[:], func=AF.Exp, scale=-0.5), rd=[t_rq], wr=[t_rq])
        op("dve", lambda h: h.scalar_tensor_tensor(out=dst_ap, in0=ps[bank][:], scalar=pcols[:, gcol:gcol + 1],
                                                   in1=rq[:], op0=ALU.mult, op1=ALU.mult),
           rd=[tps[bank], t_rq, t_const], wr=[t_dst])

    rot = {"s": 0, "p": 0, "e": 0}
    attn_q = []
    SB_ = (0, 1, 2, 6)
    PB_ = (3, 4, 5, 7)
    NE_ = 5

    def attn_class(qpad, t_q, k2, t_k, V, t_V, vb0, r, d, mcol, E, t_E, fin, n_lo, n_hi):
        mcat = cbf[:, mcol:mcol + 256]
        queue = attn_q

        def scores(n):
            a = SB_[rot["s"] % 4]
            rot["s"] += 1
            e = rot["e"] % NE_
            rot["e"] += 1
            qc = cols(qpad, slice(0, 128), r + d * 128 * n, 128, d)
            kd = cols(k2, slice(0, 128), r + d * 128 * n, 128, d)
            if n > 0:
                kp = cols(k2, slice(0, 128), r + d * 128 * (n - 1), 128, d)
                op("pe", lambda h: h.matmul(ps[a][:, 0:256], ident, mcat, start=True, stop=False),
                   rd=[t_const], wr=[tps[a]], inc=False)
                op("pe", lambda h: h.matmul(ps[a][:, 0:128], kp, qc, start=False, stop=False),
                   rd=[t_k, t_q], wr=[tps[a]], inc=False)
                op("pe", lambda h: h.matmul(ps[a][:, 128:256], kd, qc, start=False, stop=True),
                   rd=[t_k, t_q], wr=[tps[a]])
                op("act", lambda h: h.activation(out=E[e][:, 0:256], in_=ps[a][:, 0:256], func=AF.Exp, scale=0.125),
                   rd=[tps[a]], wr=[t_E[e]])
            else:
                op("pe", lambda h: h.matmul(ps[a][:, 128:256], ident, mcat[:, 128:256], start=True, stop=False),
                   rd=[t_const], wr=[tps[a]], inc=False)
                op("pe", lambda h: h.matmul(ps[a][:, 128:256], kd, qc, start=False, stop=True),
                   rd=[t_k, t_q], wr=[tps[a]])
                op("act", lambda h: h.activation(out=E[e][:, 128:256], in_=ps[a][:, 128:256], func=AF.Exp,
                                                 scale=0.125),
                   rd=[tps[a]], wr=[t_E[e]])
            return e

        def pv(n, e):
            bnk = PB_[rot["p"] % 4]
            rot["p"] += 1
            if n > 0:
                op("pe", lambda h: h.matmul(ps[bnk][:, 0:128], V[:, vb0 + n - 1, :], E[e][:, 0:128],
                                            start=True, stop=False),
                   rd=[t_V, t_E[e]], wr=[tps[bnk]], inc=False)
            op("pe", lambda h: h.matmul(ps[bnk][:, 0:128], V[:, vb0 + n, :], E[e][:, 128:256],
                                        start=(n == 0), stop=True),
               rd=[t_V, t_E[e]], wr=[tps[bnk]], inc=False)
            if n > 0:
                op("pe", lambda h: h.matmul(ps[bnk][:, 128:256], ones_b[:], E[e][:, 0:128], start=True, stop=False),
                   rd=[t_const, t_E[e]], wr=[tps[bnk]], inc=False)
            op("pe", lambda h: h.matmul(ps[bnk][:, 128:256], ones_b[:], E[e][:, 128:256], start=(n == 0), stop=True),
               rd=[t_const, t_E[e]], wr=[tps[bnk]])
            fin(n, bnk)

        for n in range(n_lo, n_hi):
            e = scores(n)
            queue.append(lambda n=n, e=e: pv(n, e))
            if len(queue) > 3:
                queue.pop(0)()

    def attn_flush():
        while attn_q:
            attn_q.pop(0)()

    def out_proj(layer):
        b0 = mix_base(layer) + (14 if layer % 2 == 0 else 16)
        with ExitStack() as es:
            sb = lambda name, shape, dt: es.enter_context(sbt(name, shape, dt))
            wout = sb("wout", [128, 8, 1024], BF16)
            t_wout = Tok()
            mixt = [sb("mixt%d" % i, [128, 8, TT], BF16) for i in range(2)]
            t_mixt = [Tok() for _ in range(2)]
            d_wout, d_m0, d_m1 = get_sems("oproj", ["wout", "mixt0", "mixt1"])
            d_mixt = [d_m0, d_m1]
            op("sp", lambda h: h.dma_start(out=wout[:], in_=wmix_b[b0:b0 + 8].rearrange("u p n -> p u n")),
               rd=[t_wmixb[layer]], wr=[t_wout], dsem=d_wout)

            def load(tt):
                b = tt % 2
                op("sp", lambda h: h.dma_start(out=mixt[b][:],
                                               in_=mixT_d[:, tt * TT:(tt + 1) * TT].rearrange("(k p) t -> p k t", p=128)),
                   rd=[t_mixd], wr=[t_mixt[b]], dsem=d_mixt[b])
            load(0)
            for tt in range(NT):
                b = tt % 2
                sl = slice(tt * TT, (tt + 1) * TT)
                if tt + 1 < NT:
                    load(tt + 1)
                for c in range(8):
                    bank = 5 + c % 2
                    proj8(bank, lambda k: wout[:, c, k * 128:(k + 1) * 128], mixt[b], t_wout, t_mixt[b])
                    if c == 0:
                        bgconv(1)
                    op("dve", lambda h: h.tensor_tensor(out=xs[:, c, sl], in0=ps[bank][:], in1=xs[:, c, sl], op=ALU.add),
                       rd=[tps[bank], tx[c][tt]], wr=[tx[c][tt]])
            kb.barrier()

    def even_mixer(layer):
        i = layer // 2
        gbase = PC_G + (layer * 3 + 1) * 8
        pc = PC_EV + i * EV_N
        b0 = mix_base(layer)
        conv_need(("m", layer))
        with ExitStack() as es:
            sb = lambda name, shape, dt: es.enter_context(sbt(name, shape, dt))
            nb = NormBufs(es, nsq=8, nxn=3)
            win = sb("win", [128, 8, 1024], BF16)
            t_win = Tok()
            sig = [sb("sig%d" % j, [128, TT], BF16) for j in range(4)]
            t_sig = [Tok() for _ in range(4)]
            ust = [sb("ust%d" % j, [128, 4, TT], BF16) for j in range(2)]
            t_ust = [Tok() for _ in range(2)]
            d_win, d_u0, d_u1 = get_sems("e1a", ["e1aw", "e1au0", "e1au1"])
            d_u = [d_u0, d_u1]
            op("sp", lambda h: h.dma_start(out=win[:], in_=wmix_b[b0:b0 + 8].rearrange("u p n -> p u n")),
               rd=[t_wmixb[layer]], wr=[t_win], dsem=d_win)
            nb.norm(0, gbase, 0, sq_eng="act")
            nb.norm(1, gbase, 1)
            for tt in range(NT):
                b = tt % 2
                sl = slice(tt * TT, (tt + 1) * TT)
                xn, t_xn = nb.xn[tt % 3], nb.t_xn[tt % 3]
                bgconv(1)
                for c in range(4):
                    bv, bg = ((1, 2), (3, 4), (5, 6))[pr_i[0] % 3]
                    pr_i[0] += 1
                    proj8(bv, lambda k: win[:, c, k * 128:(k + 1) * 128], xn, t_win, t_xn)
                    proj8(bg, lambda k: win[:, 4 + c, k * 128:(k + 1) * 128], xn, t_win, t_xn)
                    op("act", lambda h: h.activation(out=sig[c][:], in_=ps[bg][:], func=AF.Sigmoid),
                       rd=[tps[bg]], wr=[t_sig[c]])
                    op("dve", lambda h: h.tensor_tensor(out=ust[b][:, c, :], in0=sig[c][:], in1=ps[bv][:],
                                                        op=ALU.mult),
                       rd=[t_sig[c], tps[bv]], wr=[t_ust[b]])
                    if tt + 2 < NT and c in (0, 3):
                        if c == 0:
                            nb.norm_part(tt + 2, gbase, (tt + 2) % 3, 0)
                        else:
                            for part in (1, 2, 3):
                                nb.norm_part(tt + 2, gbase, (tt + 2) % 3, part)
                op("pool", lambda h: h.dma_start(out=u_d[:, sl].rearrange("(c p) t -> p c t", p=128), in_=ust[b][:]),
                   rd=[t_ust[b]], wr=[t_ud], dsem=d_u[b])
            kb.barrier()
        with ExitStack() as es:
            sb = lambda name, shape, dt: es.enter_context(sbt(name, shape, dt))
            diag = sb("diag", [128, 4, CONVW, 128], BF16)
            t_diag = [Tok() for _ in range(4)]
            ubuf = [sb("ubuf%d" % j, [128, 4, 30 + TT], BF16) for j in range(2)]
            t_ubuf = [Tok() for _ in range(2)]
            u2 = sb("u2", [128, 4, TT], F32)
            t_u2 = [Tok() for _ in range(4)]
            u2b = [sb("u2b%d" % j, [128, TT], BF16) for j in range(2)]
            t_u2b = [Tok() for _ in range(2)]
            u2q = [sb("u2q%d" % j, [128, TT], BF16) for j in range(2)]
            t_u2q = [Tok() for _ in range(2)]
            mean = sb("mean", [128, TT], F32)
            t_mean = Tok()
            msq = sb("msq", [128, TT], F32)
            t_msq = Tok()
            yb = [sb("yb%d" % j, [128, 4, TT], BF16) for j in range(2)]
            t_yb = [Tok() for _ in range(2)]
            d_ub0, d_ub1, d_y0, d_y1 = get_sems("e1b", ["e1bu0", "e1bu1", "e1by0", "e1by1"])
            d_ub = [d_ub0, d_ub1]
            d_yb = [d_y0, d_y1]
            for c in range(4):
                for j in range(CONVW):
                    col = pc + EV_CW + c * CONVW + j
                    if j % 2 == 0:
                        op("dve", lambda h: h.tensor_scalar(out=diag[:, c, j, :], in0=ident, scalar1=pcols[:, col:col + 1],
                                                            scalar2=None, op0=ALU.mult),
                           rd=[t_const], wr=[t_diag[c]])
                    else:
                        op("act", lambda h: h.activation(out=diag[:, c, j, :], in_=ident, func=AF.Identity,
                                                         scale=pcols[:, col:col + 1]),
                           rd=[t_const], wr=[t_diag[c]])
            op("pool", lambda h: h.memset(ubuf[0][:, :, 0:30], 0.0), wr=[t_ubuf[0]])

            def load_u(tt):
                b = tt % 2
                lo = tt * TT - 30
                if tt == 0:
                    op("sp", lambda h: h.dma_start(out=ubuf[b][:, :, 30:30 + TT],
                                                   in_=u_d[:, 0:TT].rearrange("(c p) t -> p c t", p=128)),
                       rd=[t_ud], wr=[t_ubuf[b]], dsem=d_ub[b])
                else:
                    op("sp", lambda h: h.dma_start(out=ubuf[b][:],
                                                   in_=u_d[:, lo:lo + 30 + TT].rearrange("(c p) t -> p c t", p=128)),
                       rd=[t_ud], wr=[t_ubuf[b]], dsem=d_ub[b])
            load_u(0)
            for tt in range(NT):
                b = tt % 2
                sl = slice(tt * TT, (tt + 1) * TT)
                if tt + 1 < NT:
                    load_u(tt + 1)
                bgconv(1)
                for c in range(4):
                    bank = 5 + c % 2
                    for j in range(CONVW):
                        op("pe", lambda h: h.matmul(ps[bank][:], diag[:, c, j, :], ubuf[b][:, c, j:j + TT],
                                                    start=(j == 0), stop=(j == CONVW - 1)),
                           rd=[t_diag[c], t_ubuf[b]], wr=[tps[bank]], inc=(j == CONVW - 1))
                    cb = pc + EV_CB + c
                    op("act", lambda h: h.activation(out=u2[:, c, :], in_=ps[bank][:], func=AF.Identity,
                                                     bias=pcols[:, cb:cb + 1], scale=1.0),
                       rd=[tps[bank], t_const], wr=[t_u2[c]])
                    op("dve", lambda h: h.tensor_copy(out=u2b[c % 2][:], in_=u2[:, c, :]),
                       rd=[t_u2[c]], wr=[t_u2b[c % 2]])
                    op("act", lambda h: h.activation(out=u2q[c % 2][:], in_=u2[:, c, :], func=AF.Square),
                       rd=[t_u2[c]], wr=[t_u2q[c % 2]])

                    def stats_mm(cc):
                        op("pe", lambda h: h.matmul(ps[7][:], ones_b[:], u2b[cc % 2][:], start=(cc == 0), stop=(cc == 3)),
                           rd=[t_u2b[cc % 2], t_const], wr=[tps[7]])
                        op("pe", lambda h: h.matmul(ps[0][:], ones_b[:], u2q[cc % 2][:], start=(cc == 0), stop=(cc == 3)),
                           rd=[t_u2q[cc % 2], t_const], wr=[tps[0]])
                    if c >= 1:
                        stats_mm(c - 1)
                    if c == 3:
                        stats_mm(3)
                op("act", lambda h: h.activation(out=mean[:], in_=ps[7][:], func=AF.Identity, scale=1.0 / 512),
                   rd=[tps[7]], wr=[t_mean])
                op("dve", lambda h: h.tensor_tensor(out=msq[:], in0=mean[:], in1=mean[:], op=ALU.mult),
                   rd=[t_mean], wr=[t_msq])
                op("dve", lambda h: h.scalar_tensor_tensor(out=msq[:], in0=ps[0][:], scalar=1.0 / 512, in1=msq[:],
                                                           op0=ALU.mult, op1=ALU.subtract),
                   rd=[tps[0], t_msq], wr=[t_msq])
                op("act", lambda h: h.activation(out=msq[:], in_=msq[:], func=AF.Ln, bias=EPS, scale=1.0),
                   rd=[t_msq], wr=[t_msq])
                op("act", lambda h: h.activation(out=msq[:], in_=msq[:], func=AF.Exp, scale=-0.5),
                   rd=[t_msq], wr=[t_msq])
                for c in range(4):
                    op("dve", lambda h: h.tensor_tensor(out=u2[:, c, :], in0=u2[:, c, :], in1=mean[:], op=ALU.subtract),
                       rd=[t_u2[c], t_mean], wr=[t_u2[c]])
                    op("dve", lambda h: h.tensor_tensor(out=u2[:, c, :], in0=u2[:, c, :], in1=msq[:], op=ALU.mult),
                       rd=[t_u2[c], t_msq], wr=[t_u2[c]])
                    lg, lb = pc + EV_LG + c, pc + EV_LB + c
                    op("act", lambda h: h.activation(out=yb[b][:, c, :], in_=u2[:, c, :], func=AF.Silu,
                                                     bias=pcols[:, lb:lb + 1], scale=pcols[:, lg:lg + 1]),
                       rd=[t_u2[c], t_const], wr=[t_yb[b]])
                op("pool", lambda h: h.dma_start(out=mixT_d[0:512, sl].rearrange("(c p) t -> p c t", p=128),
                                                 in_=yb[b][:]),
                   rd=[t_yb[b]], wr=[t_mixd], dsem=d_yb[b])
            kb.barrier()
        with ExitStack() as es:
            sb = lambda name, shape, dt: es.enter_context(sbt(name, shape, dt))
            nb = NormBufs(es, nsq=8, nxn=3)
            win = sb("win2", [128, 6, 1024], BF16)
            t_win = Tok()
            sqh = [sb("sqh%d" % j, [128, TT], BF16) for j in range(3)]
            rq = [sb("rq%d" % j, [128, TT], F32) for j in range(3)]
            t_sqh, t_rq = [Tok() for _ in range(3)], [Tok() for _ in range(3)]
            qst = [sb("qst%d" % j, [128, 5, TT], BF16) for j in range(2)]
            t_qst = [Tok() for _ in range(2)]
            vst = [sb("vst%d" % j, [128, 4, 128], BF16) for j in range(2)]
            t_vst = [Tok() for _ in range(2)]
            d_win, d_q0, d_q1, d_v0, d_v1 = get_sems("e2", ["e2w", "e2q0", "e2q1", "e2v0", "e2v1"])
            d_q = [d_q0, d_q1]
            d_v = [d_v0, d_v1]
            op("sp", lambda h: h.dma_start(out=win[:], in_=wmix_b[b0 + 8:b0 + 14].rearrange("u p n -> p u n")),
               rd=[t_wmixb[layer]], wr=[t_win], dsem=d_win)
            nb.norm(0, gbase, 0, sq_eng="act")
            nb.norm(1, gbase, 1)
            for tt in range(NT):
                b = tt % 2
                sl = slice(tt * TT, (tt + 1) * TT)
                xn, t_xn = nb.xn[tt % 3], nb.t_xn[tt % 3]
                bgconv(1)
                pend = None
                for c in range(5):
                    bank = (1, 2, 6)[pr_i[0] % 3]
                    ssb = (3, 4, 7)[pr_i[0] % 3]
                    pr_i[0] += 1
                    proj8(bank, lambda k: win[:, c, k * 128:(k + 1) * 128], xn, t_win, t_xn)
                    st = head_norm_a(bank, (sqh, t_sqh, rq, t_rq, ssb))
                    if pend is not None:
                        head_norm_b(*pend)
                    pend = (st, pc + (EV_QG if c < 4 else EV_KG), qst[b][:, c, :], t_qst[b])
                    if tt + 2 < NT and c in (0, 4):
                        if c == 0:
                            nb.norm_part(tt + 2, gbase, (tt + 2) % 3, 0)
                        else:
                            for part in (1, 2, 3):
                                nb.norm_part(tt + 2, gbase, (tt + 2) % 3, part)
                head_norm_b(*pend)
                for tb in range(4):
                    for k in range(8):
                        op("pe", lambda h: h.matmul(ps[5][:, tb * 128:(tb + 1) * 128], xn[:, k, tb * 128:(tb + 1) * 128],
                                                    win[:, 5, k * 128:(k + 1) * 128], start=(k == 0), stop=(k == 7)),
                           rd=[t_win, t_xn], wr=[tps[5]], inc=(k == 7))
                op("act", lambda h: h.copy(out=vst[b][:], in_=ps[5][:].rearrange("p (a f) -> p a f", a=4)),
                   rd=[tps[5]], wr=[t_vst[b]])
                op("pool", lambda h: h.dma_start(out=qT_d[:, sl].rearrange("(c p) t -> p c t", p=128),
                                                 in_=qst[b][:, 0:4, :]),
                   rd=[t_qst[b]], wr=[t_qd], dsem=d_q[b])
                op("pool", lambda h: h.dma_start(out=kT_d[0:128, sl], in_=qst[b][:, 4, :]),
                   rd=[t_qst[b]], wr=[t_kd], dsem=d_q[b])
                op("pool", lambda h: h.dma_start(out=v_d[sl, 0:128].rearrange("(a p) f -> p a f", p=128),
                                                 in_=vst[b][:]),
                   rd=[t_vst[b]], wr=[t_vd], dsem=d_v[b])
            kb.barrier()
        with ExitStack() as es:
            sb = lambda name, shape, dt: es.enter_context(sbt(name, shape, dt))
            k2 = sb("k2", [128, S], BF16)
            V = sb("V", [128, 32, 128], BF16)
            qp = [sb("qpA", [128, S], BF16), sb("qpB", [128, S], BF16)]
            att = sb("att", [128, S], BF16)
            E = [sb("E%d" % j, [128, 256], BF16) for j in range(5)]
            HSe = S // 2
            acce = sb("acce", [128, 2, HSe], F32)
            t_acce = Tok(True)
            es_t = sb("es", [128, 8], F32)
            t_k2, t_V, t_att, t_es = Tok(), Tok(), Tok(True), Tok()
            t_qp = [Tok(), Tok()]
            t_E = [Tok() for _ in range(5)]
            d_k, d_v2, d_qa, d_qb, d_att = get_sems("eB", ["eBk", "eBv", "eBqa", "eBqb", "eBatt"])
            d_qp = [d_qa, d_qb]
            op("sp", lambda h: h.dma_start(out=k2[:], in_=kT_d[0:128, :]), rd=[t_kd], wr=[t_k2], dsem=d_k)
            vsrc = v_d[:, 0:128].rearrange("(n p) f -> p n f", p=128)
            for q4 in range(4):
                op("sp", lambda h: h.dma_start(out=V[:, q4 * 8:(q4 + 1) * 8, :], in_=vsrc[:, q4 * 8:(q4 + 1) * 8, :]),
                   rd=[t_vd], wr=[t_V], dsem=d_v2)
            op("pool", lambda h: h.memset(qp[0][64:128, :], 0.0), wr=[t_qp[0]])
            op("pool", lambda h: h.memset(qp[1][0:64, :], 0.0), wr=[t_qp[1]])
            op("act", lambda h: h.activation(out=es_t[:], in_=pcols[:, pc + EV_SK:pc + EV_SK + 8], func=AF.Exp),
               rd=[t_const], wr=[t_es])
            for hh in range(8):
                kv = hh // 4
                rows = slice(kv * 64, kv * 64 + 64)
                op("sp", lambda h: h.dma_start(out=qp[kv][rows, :], in_=qT_d[hh * 64:(hh + 1) * 64, :]),
                   rd=[t_qd], wr=[t_qp[kv]], dsem=d_qp[kv])

                for half in range(2):
                    def fin(n, bnk, half=half, rows=rows):
                        c0 = n * 128 - half * HSe
                        op("dve", lambda h: h.tensor_copy(out=acce[rows, :, c0:c0 + 128],
                                                          in_=ps[bnk][rows, 0:256].rearrange("p (a q) -> p a q", a=2)),
                           rd=[tps[bnk]], wr=[t_acce])
                    if half == 0:
                        bgconv(1)
                    attn_class(qp[kv], t_qp[kv], k2, t_k2, V, t_V, 0, 0, 1, C_MPS, E, t_E, fin, half * 16, (half + 1) * 16)
                    attn_flush()
                    op("act", lambda h: h.activation(out=acce[rows, 1, :], in_=acce[rows, 1, :], func=AF.Ln,
                                                     bias=es_t[rows, hh:hh + 1], scale=1.0),
                       rd=[t_acce, t_es], wr=[t_acce])
                    op("act", lambda h: h.activation(out=acce[rows, 1, :], in_=acce[rows, 1, :], func=AF.Exp, scale=-1.0),
                       rd=[t_acce], wr=[t_acce])
                    op("dve", lambda h: h.tensor_tensor(out=att[rows, half * HSe:(half + 1) * HSe], in0=acce[rows, 0, :],
                                                        in1=acce[rows, 1, :], op=ALU.mult),
                       rd=[t_acce], wr=[t_att])
                op("pool", lambda h: h.dma_start(out=mixT_d[512 + hh * 64:512 + (hh + 1) * 64, :], in_=att[rows, :]),
                   rd=[t_att], wr=[t_mixd], dsem=d_att)
            kb.barrier()
        out_proj(layer)

    def odd_mixer(layer):
        i = layer // 2
        gbase = PC_G + (layer * 3 + 1) * 8
        pc = PC_OD + i * OD_N
        b0 = mix_base(layer)
        conv_need(("m", layer))
        with ExitStack() as es:
            sb = lambda name, shape, dt: es.enter_context(sbt(name, shape, dt))
            nb = NormBufs(es, nsq=8, nxn=3)
            win = sb("winA", [128, 8, 1024], BF16)
            t_win = Tok()
            sqh = [sb("sqh%d" % j, [128, TT], BF16) for j in range(3)]
            rq = [sb("rq%d" % j, [128, TT], F32) for j in range(3)]
            t_sqh, t_rq = [Tok() for _ in range(3)], [Tok() for _ in range(3)]
            qst = [sb("qst%d" % j, [128, 8, TT], BF16) for j in range(2)]
            t_qst = [Tok() for _ in range(2)]
            d_win, d_q0, d_q1 = get_sems("oA1", ["oA1w", "oA1q0", "oA1q1"])
            d_q = [d_q0, d_q1]
            op("sp", lambda h: h.dma_start(out=win[:], in_=wmix_b[b0:b0 + 8].rearrange("u p n -> p u n")),
               rd=[t_wmixb[layer]], wr=[t_win], dsem=d_win)
            nb.norm(0, gbase, 0, sq_eng="act")
            nb.norm(1, gbase, 1)
            for tt in range(NT):
                b = tt % 2
                sl = slice(tt * TT, (tt + 1) * TT)
                xn, t_xn = nb.xn[tt % 3], nb.t_xn[tt % 3]
                bgconv(1)
                pend = None
                for c in range(8):
                    bank = (1, 2, 5, 6)[pr_i[0] % 4]
                    ssb = (3, 4, 7)[pr_i[0] % 3]
                    pr_i[0] += 1
                    proj8(bank, lambda k: win[:, c, k * 128:(k + 1) * 128], xn, t_win, t_xn)
                    st = head_norm_a(bank, (sqh, t_sqh, rq, t_rq, ssb))
                    if pend is not None:
                        head_norm_b(*pend)
                    pend = (st, pc + (OD_QG if c < 4 else OD_KG), qst[b][:, c, :], t_qst[b])
                    if tt + 2 < NT and c in (0, 7):
                        if c == 0:
                            nb.norm_part(tt + 2, gbase, (tt + 2) % 3, 0)
                        else:
                            for part in (1, 2, 3):
                                nb.norm_part(tt + 2, gbase, (tt + 2) % 3, part)
                head_norm_b(*pend)
                op("pool", lambda h: h.dma_start(out=qT_d[:, sl].rearrange("(c p) t -> p c t", p=128),
                                                 in_=qst[b][:, 0:4, :]),
                   rd=[t_qst[b]], wr=[t_qd], dsem=d_q[b])
                op("pool", lambda h: h.dma_start(out=kT_d[:, sl].rearrange("(c p) t -> p c t", p=128),
                                                 in_=qst[b][:, 4:8, :]),
                   rd=[t_qst[b]], wr=[t_kd], dsem=d_q[b])
            kb.barrier()
        with ExitStack() as es:
            sb = lambda name, shape, dt: es.enter_context(sbt(name, shape, dt))
            nb = NormBufs(es)
            win = sb("winB", [128, 8, 1024], BF16)
            pw = sb("pw", [128, 4, 128], BF16)
            t_win = Tok()
            vst = [sb("vst%d" % j, [128, 4, 512], BF16) for j in range(2)]
            t_vst = [Tok() for _ in range(2)]
            ubuf = sb("ubufo", [128, 4, 16 + TT], F32)
            t_ub = [Tok() for _ in range(4)]
            tAB = [[sb("tA%d" % j, [128, 16 + TT], F32), sb("tB%d" % j, [128, 16 + TT], F32)] for j in range(2)]
            t_tAB = [[Tok(), Tok()] for j in range(2)]
            pb = [sb("pb%d" % j, [128, TT], BF16) for j in range(4)]
            t_pb = [Tok() for _ in range(4)]
            pst = [sb("pst%d" % j, [128, 4, TT], BF16) for j in range(2)]
            t_pst = [Tok() for _ in range(2)]
            d_win, d_v0, d_v1, d_p0, d_p1 = get_sems("oA2", ["oA2w", "oA2v0", "oA2v1", "oA2p0", "oA2p1"])
            d_v = [d_v0, d_v1]
            d_p = [d_p0, d_p1]
            op("sp", lambda h: h.dma_start(out=win[:], in_=wmix_b[b0 + 8:b0 + 16].rearrange("u p n -> p u n")),
               rd=[t_wmixb[layer]], wr=[t_win], dsem=d_win)
            op("sp", lambda h: h.dma_start(out=pw[:].rearrange("p g c -> p (g c)"), in_=wmix_b[b0 + 24][:, 0:512]),
               rd=[t_wmixb[layer]], wr=[t_win], dsem=d_win)
            op("pool", lambda h: h.memset(ubuf[:, :, 0:16], 0.0), wr=t_ub)
            for j in range(2):
                for q in range(2):
                    op("pool", lambda h: h.memset(tAB[j][q][:], 0.0), wr=[t_tAB[j][q]])
            nb.norm(0, gbase, 0)
            W = 16 + TT

            def pool_mm(ttp):
                bp = ttp % 2
                for g in range(4):
                    op("pe", lambda h: h.matmul(ps[7][:], pw[:, g, :], pb[g][:], start=True, stop=True),
                       rd=[t_win, t_pb[g]], wr=[tps[7]])
                    sc = pc + OD_PS + g
                    op("act", lambda h: h.activation(out=pst[bp][:, g, :], in_=ps[7][:], func=AF.Identity,
                                                     scale=pcols[:, sc:sc + 1]),
                       rd=[tps[7], t_const], wr=[t_pst[bp]])
                op("pool", lambda h: h.dma_start(
                    out=mixT_d[512:1024, ttp * TT:(ttp + 1) * TT].rearrange("(c p) t -> p c t", p=128), in_=pst[bp][:]),
                   rd=[t_pst[bp]], wr=[t_mixd], dsem=d_p[bp])

            for tt in range(NT):
                b = tt % 2
                sl = slice(tt * TT, (tt + 1) * TT)
                xn, t_xn = nb.xn[b], nb.t_xn[b]
                bgconv(1)
                for tb in range(4):
                    bank = 5 + tb % 2
                    for k in range(8):
                        op("pe", lambda h: h.matmul(ps[bank][:], xn[:, k, tb * 128:(tb + 1) * 128],
                                                    win[:, k // 2, (k % 2) * 512:(k % 2) * 512 + 512],
                                                    start=(k == 0), stop=(k == 7)),
                           rd=[t_win, t_xn], wr=[tps[bank]], inc=(k == 7))
                    if tb % 2 == 0:
                        op("act", lambda h: h.copy(out=vst[b][:, tb, :], in_=ps[bank][:]),
                           rd=[tps[bank]], wr=[t_vst[b]])
                    else:
                        op("dve", lambda h: h.tensor_copy(out=vst[b][:, tb, :], in_=ps[bank][:]),
                           rd=[tps[bank]], wr=[t_vst[b]])
                    if tb == 0 and tt + 1 < NT:
                        nb.norm(tt + 1, gbase, 1 - b, sq_eng="act")
                op("pool", lambda h: h.dma_start(out=v_d[sl, :].rearrange("(a p) f -> p a f", p=128), in_=vst[b][:]),
                   rd=[t_vst[b]], wr=[t_vd], dsem=d_v[b])
                for g in range(4):
                    bank = 1 + g % 2
                    proj8(bank, lambda k: win[:, 4 + g, k * 128:(k + 1) * 128], xn, t_win, t_xn)
                    op("act", lambda h: h.copy(out=ubuf[:, g, 16:W], in_=ps[bank][:]),
                       rd=[tps[bank]], wr=[t_ub[g]])
                if tt > 0:
                    pool_mm(tt - 1)
                for g in range(4):
                    w = POOL_SIZES[g]
                    ug = ubuf[:, g, :]
                    src, t_src = ug, t_ub[g]
                    bufs = [(tAB[g % 2][0], t_tAB[g % 2][0]), (tAB[g % 2][1], t_tAB[g % 2][1])]
                    sh = 1
                    step = 0
                    while sh < w:
                        dst, t_dst = bufs[step % 2]
                        op("dve", lambda h: h.tensor_tensor(out=dst[:, sh:W], in0=src[:, sh:W], in1=src[:, 0:W - sh],
                                                            op=ALU.add),
                           rd=[t_src], wr=[t_dst])
                        src, t_src = dst, t_dst
                        sh *= 2
                        step += 1
                    j = g
                    op("dve", lambda h: h.scalar_tensor_tensor(out=pb[j][:], in0=src[:, 16:W], scalar=1.0 / w,
                                                               in1=ug[:, 16:W], op0=ALU.mult, op1=ALU.subtract),
                       rd=[t_src, t_ub[g]], wr=[t_pb[j]])
                    if tt == 0:
                        ic = icnt[:, g * 16:(g + 1) * 16]
                        op("dve", lambda h: h.tensor_tensor(out=src[:, 16:32], in0=src[:, 16:32], in1=ic, op=ALU.mult),
                           rd=[t_src, t_const], wr=[t_src])
                        op("dve", lambda h: h.tensor_tensor(out=pb[j][:, 0:16], in0=src[:, 16:32], in1=ug[:, 16:32],
                                                            op=ALU.subtract),
                           rd=[t_src, t_ub[g]], wr=[t_pb[j]])
                    op("pool", lambda h: h.tensor_copy(out=ubuf[:, g, 0:16], in_=ubuf[:, g, TT:TT + 16]),
                       rd=[t_ub[g]], wr=[t_ub[g]])
            pool_mm(NT - 1)
            kb.barrier()
        with ExitStack() as es:
            sb = lambda name, shape, dt: es.enter_context(sbt(name, shape, dt))
            k2b = [sb("k2_%d" % j, [128, S], BF16) for j in range(2)]
            t_k2b = [Tok(), Tok()]
            Vb = [sb("V%d" % j, [128, 32, 128], BF16) for j in range(2)]
            qp = [sb("qpA", [128, S], BF16), sb("qpB", [128, S], BF16)]
            HS = S // 2
            acc = sb("acc", [128, 2, HS], F32)
            att = sb("atto", [128, HS], BF16)
            E = [sb("E%d" % j, [128, 256], BF16) for j in range(5)]
            t_att = Tok()
            t_accb = [[Tok(True) for _ in range(3)] for _ in range(2)]
            t_Vb = [Tok(), Tok()]
            t_qp = [Tok(), Tok()]
            t_E = [Tok() for _ in range(5)]
            d_k, d_va, d_vb, d_qa, d_qb, d_att = get_sems("oB", ["oBk", "oBva", "oBvb", "oBqa", "oBqb", "oBatt"])
            d_V = [d_va, d_vb]
            d_qp = [d_qa, d_qb]
            op("pool", lambda h: h.memset(qp[0][64:128, :], 0.0), wr=[t_qp[0]])
            op("pool", lambda h: h.memset(qp[1][0:64, :], 0.0), wr=[t_qp[1]])
            vi = [0]

            def load_V(c, d):
                j = vi[0] % 2
                vi[0] += 1
                src = v_d[:, c * 128:(c + 1) * 128].rearrange("(n p r) f -> p r n f", p=128, r=d)
                nb_ = 32 // d
                for r in range(d):
                    step = min(nb_, 8)
                    for n0 in range(0, nb_, step):
                        op("sp", lambda h: h.dma_start(out=Vb[j][:, r * nb_ + n0:r * nb_ + n0 + step, :],
                                                       in_=src[:, r, n0:n0 + step, :]),
                           rd=[t_vd], wr=[t_Vb[j]], dsem=d_V[j])
                return j

            d_k2 = get_sems("oBk2", ["oBk2"])[0]

            def load_k2(cc):
                op("sp", lambda h: h.dma_start(out=k2b[cc % 2][:], in_=kT_d[cc * 128:(cc + 1) * 128, :]),
                   rd=[t_kd], wr=[t_k2b[cc % 2]], dsem=(d_k if cc % 2 == 0 else d_k2))
            load_k2(0)
            for c in range(4):
                k2, t_k2 = k2b[c % 2], t_k2b[c % 2]
                if c + 1 < 4:
                    load_k2(c + 1)
                for hp in range(2):
                    hh = 2 * c + hp
                    rows = slice(hp * 64, hp * 64 + 64)
                    op("sp", lambda h: h.dma_start(out=qp[hp][rows, :], in_=qT_d[hh * 64:(hh + 1) * 64, :]),
                       rd=[t_qd], wr=[t_qp[hp]], dsem=d_qp[hp])
                for half in range(2):
                    for bi, (wnd, d) in enumerate(DIL):
                        vj = load_V(c, d)
                        bgconv(1)
                        nblk = 32 // d
                        per_half = nblk // 2
                        for hp in range(2):
                            rows = slice(hp * 64, hp * 64 + 64)
                            for r in range(d):
                                def fin(n, bnk, r=r, d=d, half=half, bi=bi, hp=hp, rows=rows):
                                    c0 = r + d * 128 * n - half * HS
                                    dst = cols3(acc, rows, c0, 128, d)
                                    src = ps[bnk][rows, 0:256].rearrange("p (a q) -> p a q", a=2)
                                    if bi == 0:
                                        op("dve", lambda h: h.tensor_copy(out=dst, in_=src),
                                           rd=[tps[bnk]], wr=[t_accb[hp][0]])
                                    else:
                                        op("dve", lambda h: h.tensor_tensor(out=dst, in0=src, in1=dst, op=ALU.add),
                                           rd=[tps[bnk], t_accb[hp][bi - 1]], wr=[t_accb[hp][bi]])
                                attn_class(qp[hp], t_qp[hp], k2, t_k2, Vb[vj], t_Vb[vj], r * nblk, r, d, C_MPD,
                                           E, t_E, fin, half * per_half, (half + 1) * per_half)
                    attn_flush()
                    for hp in range(2):
                        rows = slice(hp * 64, hp * 64 + 64)
                        op("act", lambda h: h.activation(out=acc[rows, 1, :], in_=acc[rows, 1, :], func=AF.Ln),
                           rd=t_accb[hp], wr=[t_accb[hp][2]])
                        op("act", lambda h: h.activation(out=acc[rows, 1, :], in_=acc[rows, 1, :], func=AF.Exp, scale=-1.0),
                           rd=t_accb[hp], wr=[t_accb[hp][2]])
                        op("dve", lambda h: h.tensor_tensor(out=att[rows, :], in0=acc[rows, 0, :], in1=acc[rows, 1, :],
                                                            op=ALU.mult),
                           rd=t_accb[hp], wr=[t_att])
                    op("pool", lambda h: h.dma_start(out=mixT_d[c * 128:(c + 1) * 128, half * HS:(half + 1) * HS],
                                                     in_=att[:]),
                       rd=[t_att], wr=[t_mixd], dsem=d_att)
            kb.barrier()
        out_proj(layer)

    phases = []
    for layer in range(DEPTH):
        phases.append((layer, "ffn1"))
        phases.append((layer, "mix"))
        phases.append((layer, "ffn2"))
    if stop_after is not None:
        phases = phases[:phases.index(stop_after) + 1]
    phases = [p for p in phases if p not in skip]
    need_ffn = sorted({l * 2 + (0 if p == "ffn1" else 1) for l, p in phases if p != "mix"})
    need_mix = sorted({l for l, p in phases if p == "mix"})
    conv_plan(phases)
    for layer, p in phases:
        if p == "ffn1":
            ffn(layer * 2, layer, 0)
        elif p == "ffn2":
            ffn(layer * 2 + 1, layer, 1)
        elif layer % 2 == 0:
            even_mixer(layer)
        else:
            odd_mixer(layer)

    ds_o = kb.dsem("o")
    for c in range(8):
        for t in range(NT):
            op("sp", lambda h: h.dma_start(out=y_out[c * 128:(c + 1) * 128, t * TT:(t + 1) * TT],
                                           in_=xs[:, c, t * TT:(t + 1) * TT]),
               rd=[tx[c][t]], dsem=ds_o)
    nc.sync.wait_ge(ds_o.h, ds_o.n)
    print("built: ins=%d waits=%d" % (kb.nins, kb.nwait), {e: s.n for e, s in kb.prog.items()},
          "dsems=%d" % len(kb._dsems))
    return nc


def prep_inputs(norm_g, ffn_w_gate, ffn_w_up, ffn_w_down,
                ev_w_in, ev_w_out, ev_conv_w, ev_conv_b, ev_ln_g, ev_ln_b,
                ev_q_norm_g, ev_k_norm_g, ev_sinks,
                od_w_in, od_w_out, od_q_norm_g, od_k_norm_g, od_pool_w, od_pool_scale):
    f32 = np.float32
    wgu = np.zeros((DEPTH * 2, NM, 128, 2, 8, 128), f32)
    wd = np.zeros((DEPTH * 2, 8, 128, NM, 128), f32)
    for l in range(DEPTH):
        for w in range(2):
            f = l * 2 + w
            for g, src in ((0, ffn_w_gate), (1, ffn_w_up)):
                wp = np.zeros((D, DFFP), f32)
                wp[:, :DFF] = src[l, w]
                wgu[f, :, :, g] = wp.reshape(8, 128, NM, 128).transpose(2, 1, 0, 3)
            dp = np.zeros((DFFP, D), f32)
            dp[:DFF] = ffn_w_down[l, w]
            wd[f] = dp.reshape(NM, 128, 8, 128).transpose(2, 1, 0, 3)

    def img(w):
        n = w.shape[1] // 128
        return w.reshape(8, 128, n, 128).transpose(2, 1, 0, 3).reshape(n, 128, 1024)

    wmix = np.zeros((N_UNITS, 128, 1024), f32)
    for l in range(DEPTH):
        i = l // 2
        b0 = mix_base(l)
        if l % 2 == 0:
            wmix[b0:b0 + 14] = img(ev_w_in[i])
            wmix[b0 + 14:b0 + 22] = img(ev_w_out[i])
        else:
            wi = od_w_in[i]
            wmix[b0:b0 + 8] = img(wi[:, 0:1024])
            v = wi[:, 1024:1536].reshape(8, 128, 512).transpose(1, 0, 2)
            wmix[b0 + 8:b0 + 12] = v.reshape(128, 4, 1024).transpose(1, 0, 2)
            wmix[b0 + 12:b0 + 16] = img(wi[:, 1536:2048])
            wmix[b0 + 16:b0 + 24] = img(od_w_out[i])
            wmix[b0 + 24, :, 0:512] = od_pool_w[i].transpose(1, 0, 2).reshape(128, 512)
    pcols = np.zeros((128, PC_N), f32)
    pcols[:, PC_G:PC_G + 96] = norm_g.reshape(DEPTH * 3, 8, 128).transpose(2, 0, 1).reshape(128, 96)
    for i in range(2):
        pc = PC_EV + i * EV_N
        pcols[:, pc + EV_CW:pc + EV_CW + 124] = ev_conv_w[i].reshape(CONVW, 4, 128).transpose(2, 1, 0).reshape(128, 124)
        pcols[:, pc + EV_CB:pc + EV_CB + 4] = ev_conv_b[i].reshape(4, 128).T
        pcols[:, pc + EV_LG:pc + EV_LG + 4] = ev_ln_g[i].reshape(4, 128).T
        pcols[:, pc + EV_LB:pc + EV_LB + 4] = ev_ln_b[i].reshape(4, 128).T
        pcols[:, pc + EV_QG] = np.tile(ev_q_norm_g[i], 2)
        pcols[:, pc + EV_KG] = np.tile(ev_k_norm_g[i], 2)
        pcols[:, pc + EV_SK:pc + EV_SK + 8] = ev_sinks[i][None, :]
        po = PC_OD + i * OD_N
        pcols[:, po + OD_QG] = np.tile(od_q_norm_g[i], 2)
        pcols[:, po + OD_KG] = np.tile(od_k_norm_g[i], 2)
        pcols[:, po + OD_PS:po + OD_PS + 4] = od_pool_scale[i].reshape(4, 128).T
    consts = np.zeros((128, CN), f32)
    kk = np.arange(128)[:, None]
    qq = np.arange(128)[None, :]
    consts[:, C_ID:C_ID + 128] = np.eye(128, dtype=f32)
    consts[:, C_MPS:C_MPS + 128] = np.where(kk > qq, 0.0, NEG)
    consts[:, C_MD:C_MD + 128] = np.where(kk <= qq, 0.0, NEG)
    consts[:, C_MPD:C_MPD + 128] = np.where(kk >= qq, 0.0, NEG)
    consts[:, C_MD2:C_MD2 + 128] = np.where(kk <= qq, 0.0, NEG)
    consts[:, C_OB:C_OB + 128] = (kk // 64 == qq // 64).astype(f32)
    for g, w in enumerate(POOL_SIZES):
        consts[:, C_IC + g * 16:C_IC + (g + 1) * 16] = (1.0 / np.minimum(np.arange(1, 17), w)).astype(f32)[None, :]
    return dict(wgu=wgu.reshape(DEPTH * 2 * NM, 128, 2048), wd=wd.reshape(DEPTH * 2 * 8, 128, NM * 128),
                wmix=wmix, pcols=pcols, consts=consts)


def kernel(x, norm_g, ffn_w_gate, ffn_w_up, ffn_w_down,
           ev_w_in, ev_w_out, ev_conv_w, ev_conv_b, ev_ln_g, ev_ln_b,
           ev_q_norm_g, ev_k_norm_g, ev_sinks,
           od_w_in, od_w_out, od_q_norm_g, od_k_norm_g, od_pool_w, od_pool_scale,
           _n_cores=NB, _stop_after=None, _skip=(), _trace=False):
    x = np.asarray(x, np.float32)
    a = lambda v: np.asarray(v, np.float32)
    shared = prep_inputs(a(norm_g), a(ffn_w_gate), a(ffn_w_up), a(ffn_w_down),
                         a(ev_w_in), a(ev_w_out), a(ev_conv_w), a(ev_conv_b), a(ev_ln_g), a(ev_ln_b),
                         a(ev_q_norm_g), a(ev_k_norm_g), a(ev_sinks),
                         a(od_w_in), a(od_w_out), a(od_q_norm_g), a(od_k_norm_g), a(od_pool_w), a(od_pool_scale))
    nc = build_program(stop_after=_stop_after, skip=_skip)
    in_maps = []
    for b in range(_n_cores):
        m = dict(shared)
        m["xT"] = np.ascontiguousarray(x[b].T)
        in_maps.append(m)
    if _trace:
        res = run_bass_kernel_spmd(nc, in_maps, core_ids=list(range(_n_cores)), trace=True)
        print("exec_time_ns", res.exec_time_ns)
    else:
        res = run_bass_kernel_spmd(nc, in_maps, core_ids=list(range(_n_cores)))
    out = np.stack([np.ascontiguousarray(r["yT"].T) for r in res.results], axis=0)
    return out.astype(np.float32)
```

```python
import numpy as np
from contextlib import ExitStack
import concourse.bass as bass
import concourse.mybir as mybir
from concourse.bass_utils import run_bass_kernel_spmd

F32 = mybir.dt.float32
BF16 = mybir.dt.bfloat16
AF = mybir.ActivationFunctionType
ALU = mybir.AluOpType

D = 1024
S = 4096
NB = 8
DEPTH = 4
DFF = 2752
DFFP = 2816
NM = 22
TT = 512
NT = S // TT
EPS = 1e-6
CONVW = 31
NEG = -30000.0
POOL_SIZES = (2, 4, 8, 16)
DIL = ((128, 1), (512, 4), (2048, 16))

PC_G = 0
PC_EV = 96
EV_CW, EV_CB, EV_LG, EV_LB, EV_QG, EV_KG, EV_SK = 0, 124, 128, 132, 136, 137, 138
EV_N = 146
PC_OD = PC_EV + 2 * EV_N
OD_QG, OD_KG, OD_PS = 0, 1, 2
OD_N = 6
PC_N = PC_OD + 2 * OD_N
C_ID, C_MPS, C_MD, C_MPD, C_MD2, C_OB = 0, 128, 256, 384, 512, 640
CBF = 768
C_IC = 768
CN = 832
EV_UNITS = 22
OD_UNITS = 25


def mix_base(layer):
    i = layer // 2
    return i * (EV_UNITS + OD_UNITS) + (0 if layer % 2 == 0 else EV_UNITS)


N_UNITS = 2 * (EV_UNITS + OD_UNITS)


class Tok:
    __slots__ = ("w", "r", "multi")

    def __init__(self, multi=False):
        self.w = {}
        self.r = {}
        self.multi = multi


class Sem:
    def __init__(self, h, dma=False):
        self.h = h
        self.n = 0
        self.dma = dma


class KB:
    def __init__(self, nc):
        self.nc = nc
        self.E = dict(pe=nc.tensor, act=nc.scalar, dve=nc.vector, pool=nc.gpsimd, sp=nc.sync)
        self.prog = {e: Sem(nc.alloc_semaphore("prog_" + e)) for e in self.E}
        self.seen = {e: {} for e in self.E}
        self.nwait = 0
        self.nins = 0
        self._dsems = []

    def dsem(self, name):
        s = Sem(self.nc.alloc_semaphore("d_" + name), dma=True)
        self._dsems.append(s)
        return s

    def barrier(self):
        sems = list(self.prog.values()) + self._dsems
        for eng, h in self.E.items():
            seen = self.seen[eng]
            for s in sems:
                if s is self.prog[eng] or s.n == 0:
                    continue
                if seen.get(s, 0) < s.n:
                    h.wait_ge(s.h, s.n)
                    seen[s] = s.n
                    self.nwait += 1

    def op(self, eng, fn, rd=(), wr=(), dsem=None, inc=True):
        deps = {}

        def add(s, v):
            if s.dma:
                v = s.n
            if deps.get(s, 0) < v:
                deps[s] = v

        for t in rd:
            for s, v in t.w.items():
                add(s, v)
        for t in wr:
            if not t.multi:
                for s, v in t.w.items():
                    add(s, v)
            for s, v in t.r.items():
                add(s, v)
        seen = self.seen[eng]
        h = self.E[eng]
        mine = self.prog[eng]
        for s, v in deps.items():
            if eng == "pe" and s is mine:
                continue
            if seen.get(s, 0) < v:
                assert v < 65000, "semaphore count too large"
                h.wait_ge(s.h, v)
                seen[s] = v
                self.nwait += 1
        ins = fn(h)
        self.nins += 1
        if dsem is not None:
            dsem.n += 16
            ins.then_inc(dsem.h, 16)
            mark = (dsem, dsem.n)
        else:
            if inc:
                mine.n += 1
                ins.then_inc(mine.h, 1)
                mark = (mine, mine.n)
            else:
                mark = (mine, mine.n + 1)
        for t in rd:
            if t.r.get(mark[0], 0) < mark[1]:
                t.r[mark[0]] = mark[1]
        for t in wr:
            if t.multi:
                t.w[mark[0]] = mark[1]
            else:
                t.w = {mark[0]: mark[1]}
                t.r = {}
        return ins


def cols(buf, rows, start, count, step):
    if step == 1:
        return buf[rows, start:start + count]
    return buf[rows, start:start + step * (count - 1) + 1:step]


def cols3(buf, rows, start, count, step):
    if step == 1:
        return buf[rows, :, start:start + count]
    return buf[rows, :, start:start + step * (count - 1) + 1:step]


def build_program(stop_after=None, skip=()):
    nc = bass.Bass("TRN2", target_bir_lowering=False)
    kb = KB(nc)
    op = kb.op
    uid = [0]

    def sbt(name, shape, dt):
        uid[0] += 1
        return nc.sbuf_tensor("%s_%d" % (name, uid[0]), shape, dt)

    x_in = nc.dram_tensor("xT", [D, S], F32, kind="ExternalInput").ap()
    y_out = nc.dram_tensor("yT", [D, S], F32, kind="ExternalOutput").ap()
    pcols_in = nc.dram_tensor("pcols", [128, PC_N], F32, kind="ExternalInput").ap()
    consts_in = nc.dram_tensor("consts", [128, CN], F32, kind="ExternalInput").ap()
    wgu_in = nc.dram_tensor("wgu", [DEPTH * 2 * NM, 128, 2048], F32, kind="ExternalInput").ap()
    wd_in = nc.dram_tensor("wd", [DEPTH * 2 * 8, 128, NM * 128], F32, kind="ExternalInput").ap()
    wmix_in = nc.dram_tensor("wmix", [N_UNITS, 128, 1024], F32, kind="ExternalInput").ap()
    wgu_b = nc.dram_tensor("wgu_b", [DEPTH * 2 * NM, 128, 2048], BF16).ap()
    wd_b = nc.dram_tensor("wd_b", [DEPTH * 2 * 8, 128, NM * 128], BF16).ap()
    wmix_b = nc.dram_tensor("wmix_b", [N_UNITS, 128, 1024], BF16).ap()
    qT_d = nc.dram_tensor("qT_d", [512, S], BF16).ap()
    kT_d = nc.dram_tensor("kT_d", [512, S], BF16).ap()
    v_d = nc.dram_tensor("v_d", [S, 512], BF16).ap()
    u_d = nc.dram_tensor("u_d", [512, S], BF16).ap()
    mixT_d = nc.dram_tensor("mixT_d", [D, S], BF16).ap()
    t_qd, t_kd, t_vd, t_ud, t_mixd = Tok(True), Tok(True), Tok(True), Tok(True), Tok(True)

    xs = nc.alloc_sbuf_tensor("xs", [128, 8, S], F32)
    pcols = nc.alloc_sbuf_tensor("pcols_sb", [128, PC_N], F32)
    icnt = nc.alloc_sbuf_tensor("icnt", [128, 64], F32)
    cbf = nc.alloc_sbuf_tensor("cbf", [128, CBF], BF16)
    ones_b = nc.alloc_sbuf_tensor("ones_b", [128, 128], BF16)
    ident = cbf[:, C_ID:C_ID + 128]
    ones_blk = cbf[:, C_OB:C_OB + 128]
    tx = [[Tok() for t in range(NT)] for c in range(8)]
    t_const = Tok()
    ps = [nc.alloc_psum_tensor("ps%d" % i, [128, 512], F32) for i in range(8)]
    tps = [Tok() for i in range(8)]

    ds_xt = [kb.dsem("x%d" % t) for t in range(NT)]
    ds_c = kb.dsem("c")
    for t in range(NT):
        for c in range(8):
            op("sp", lambda h: h.dma_start(out=xs[:, c, t * TT:(t + 1) * TT],
                                           in_=x_in[c * 128:(c + 1) * 128, t * TT:(t + 1) * TT]),
               wr=[tx[c][t]], dsem=ds_xt[t])
    with sbt("c32", [128, CBF], F32) as c32:
        op("sp", lambda h: h.dma_start(out=pcols[:], in_=pcols_in[:, :]), wr=[t_const], dsem=ds_c)
        op("sp", lambda h: h.dma_start(out=c32[:], in_=consts_in[:, 0:CBF]), wr=[t_const], dsem=ds_c)
        op("sp", lambda h: h.dma_start(out=icnt[:], in_=consts_in[:, C_IC:C_IC + 64]), wr=[t_const], dsem=ds_c)
        op("dve", lambda h: h.memset(ones_b[:], 1.0), wr=[t_const])
        op("dve", lambda h: h.tensor_copy(out=cbf[:], in_=c32[:]), rd=[t_const], wr=[t_const])
        kb.barrier()

    t_wgub = [Tok(True) for f in range(DEPTH * 2)]
    t_wgub0 = [Tok(True) for m in range(4)]
    t_wdb = [Tok(True) for f in range(DEPTH * 2)]
    t_wmixb = [Tok(True) for l in range(DEPTH)]
    conv_q = []
    conv_sems = {}

    def conv_add(key, src, dst, tdst):
        if key not in conv_sems:
            conv_sems[key] = kb.dsem("cv_%s" % str(key))
        sem = conv_sems[key]
        conv_q.append((key, lambda: op("pool", lambda h: h.dma_start(out=dst, in_=src), wr=[tdst], dsem=sem)))

    def conv_plan(phases):
        for layer, p in phases:
            if p == "mix":
                b0 = mix_base(layer)
                nu = EV_UNITS if layer % 2 == 0 else OD_UNITS
                u = 0
                while u < nu:
                    k = min(2, nu - u)
                    conv_add(("m", layer), wmix_in[b0 + u:b0 + u + k].rearrange("u p n -> p u n"),
                             wmix_b[b0 + u:b0 + u + k].rearrange("u p n -> p u n"), t_wmixb[layer])
                    u += k
            else:
                f = layer * 2 + (0 if p == "ffn1" else 1)
                for m in range(NM):
                    if f == 0 and m < 4:
                        conv_add(("f0", m), wgu_in[m], wgu_b[m], t_wgub0[m])
                    else:
                        conv_add(("f", f), wgu_in[f * NM + m], wgu_b[f * NM + m], t_wgub[f])
                H = NM * 64
                for c in range(8):
                    for hh in range(2):
                        conv_add(("f", f), wd_in[f * 8 + c][:, hh * H:(hh + 1) * H],
                                 wd_b[f * 8 + c][:, hh * H:(hh + 1) * H], t_wdb[f])

    def bgconv(n=1):
        for _ in range(n):
            if conv_q:
                conv_q.pop(0)[1]()

    def conv_need(key):
        last = -1
        for i, (k, _) in enumerate(conv_q):
            if k == key:
                last = i
        for _ in range(last + 1):
            conv_q.pop(0)[1]()

    def rms_tile(tt, gbase, xn_ap, t_xn, sq, t_sq, rstd, t_rstd, pn=0, sq_eng="pool"):
        sl = slice(tt * TT, (tt + 1) * TT)
        for k in range(8):
            j = k % len(sq)
            if sq_eng == "pool":
                op("pool", lambda h: h.tensor_tensor(out=sq[j][:], in0=xs[:, k, sl], in1=xs[:, k, sl], op=ALU.mult),
                   rd=[tx[k][tt]], wr=[t_sq[j]])
            else:
                op("act", lambda h: h.activation(out=sq[j][:], in_=xs[:, k, sl], func=AF.Square),
                   rd=[tx[k][tt]], wr=[t_sq[j]])
            op("pe", lambda h: h.matmul(ps[pn][:], ones_b[:], sq[j][:], start=(k == 0), stop=(k == 7)),
               rd=[t_sq[j], t_const], wr=[tps[pn]])
        op("act", lambda h: h.activation(out=rstd[:], in_=ps[pn][:], func=AF.Ln, scale=1.0 / D, bias=EPS),
           rd=[tps[pn]], wr=[t_rstd])
        op("act", lambda h: h.activation(out=rstd[:], in_=rstd[:], func=AF.Exp, scale=-0.5), rd=[t_rstd], wr=[t_rstd])
        for k in range(8):
            op("dve", lambda h: h.scalar_tensor_tensor(out=xn_ap[:, k, :], in0=xs[:, k, sl],
                                                       scalar=pcols[:, gbase + k:gbase + k + 1],
                                                       in1=rstd[:], op0=ALU.mult, op1=ALU.mult),
               rd=[tx[k][tt], t_rstd, t_const], wr=[t_xn])

    def rms_stage(tt, gbase, xn_ap, t_xn, sq, t_sq, rstd, t_rstd, stage, pn=0):
        sl = slice(tt * TT, (tt + 1) * TT)
        if stage < 8:
            k = stage
            j = k % len(sq)
            op("act", lambda h: h.activation(out=sq[j][:], in_=xs[:, k, sl], func=AF.Square),
               rd=[tx[k][tt]], wr=[t_sq[j]])
            op("pe", lambda h: h.matmul(ps[pn][:], ones_b[:], sq[j][:], start=(k == 0), stop=(k == 7)),
               rd=[t_sq[j], t_const], wr=[tps[pn]])
        elif stage == 8:
            op("act", lambda h: h.activation(out=rstd[:], in_=ps[pn][:], func=AF.Ln, scale=1.0 / D, bias=EPS),
               rd=[tps[pn]], wr=[t_rstd])
        elif stage == 9:
            op("act", lambda h: h.activation(out=rstd[:], in_=rstd[:], func=AF.Exp, scale=-0.5),
               rd=[t_rstd], wr=[t_rstd])
        else:
            k = stage - 10
            op("dve", lambda h: h.scalar_tensor_tensor(out=xn_ap[:, k, :], in0=xs[:, k, sl],
                                                       scalar=pcols[:, gbase + k:gbase + k + 1],
                                                       in1=rstd[:], op0=ALU.mult, op1=ALU.mult),
               rd=[tx[k][tt], t_rstd, t_const], wr=[t_xn])

    class NormBufs:
        def __init__(self, es, nsq=3, nxn=2):
            self.xn = [es.enter_context(sbt("xn%d" % i, [128, 8, TT], BF16)) for i in range(nxn)]
            self.t_xn = [Tok() for _ in range(nxn)]
            self.sq = [es.enter_context(sbt("sq%d" % i, [128, TT], BF16)) for i in range(nsq)]
            self.t_sq = [Tok() for _ in range(nsq)]
            self.rstd = es.enter_context(sbt("rstd", [128, TT], F32))
            self.t_rstd = Tok()

        def norm(self, tt, gbase, b, sq_eng="pool"):
            rms_tile(tt, gbase, self.xn[b], self.t_xn[b], self.sq, self.t_sq, self.rstd, self.t_rstd, sq_eng=sq_eng)

        def norm_part(self, tt, gbase, b, part):
            sl = slice(tt * TT, (tt + 1) * TT)
            xn_ap, t_xn, sq, t_sq, rstd, t_rstd = self.xn[b], self.t_xn[b], self.sq, self.t_sq, self.rstd, self.t_rstd
            if part == 0:
                for k in range(8):
                    op("pool", lambda h: h.tensor_tensor(out=sq[k][:], in0=xs[:, k, sl], in1=xs[:, k, sl], op=ALU.mult),
                       rd=[tx[k][tt]], wr=[t_sq[k]])
            elif part == 1:
                for k in range(8):
                    op("pe", lambda h: h.matmul(ps[0][:], ones_b[:], sq[k][:], start=(k == 0), stop=(k == 7)),
                       rd=[t_sq[k], t_const], wr=[tps[0]], inc=(k == 7))
                op("act", lambda h: h.activation(out=rstd[:], in_=ps[0][:], func=AF.Ln, scale=1.0 / D, bias=EPS),
                   rd=[tps[0]], wr=[t_rstd])
                op("act", lambda h: h.activation(out=rstd[:], in_=rstd[:], func=AF.Exp, scale=-0.5),
                   rd=[t_rstd], wr=[t_rstd])
            else:
                for k in range((part - 2) * 4, (part - 2) * 4 + 4):
                    op("dve", lambda h: h.scalar_tensor_tensor(out=xn_ap[:, k, :], in0=xs[:, k, sl],
                                                               scalar=pcols[:, gbase + k:gbase + k + 1],
                                                               in1=rstd[:], op0=ALU.mult, op1=ALU.mult),
                       rd=[tx[k][tt], t_rstd, t_const], wr=[t_xn])

        def norm_stage(self, tt, gbase, b, stage):
            rms_stage(tt, gbase, self.xn[b], self.t_xn[b], self.sq, self.t_sq, self.rstd, self.t_rstd, stage)

    shared = {}

    def get_sems(key, names):
        if key not in shared:
            shared[key] = [kb.dsem(n) for n in names]
        return shared[key]

    def proj8(bank, w_ap_fn, xn, t_w, t_xn):
        for k in range(8):
            op("pe", lambda h: h.matmul(ps[bank][:], w_ap_fn(k), xn[:, k, :], start=(k == 0), stop=(k == 7)),
               rd=[t_w, t_xn], wr=[tps[bank]], inc=(k == 7))

    def ffn(f, layer, which):
        gbase = PC_G + (layer * 3 + (0 if which == 0 else 2)) * 8
        conv_need(("f", f))
        with ExitStack() as es:
            sb = lambda name, shape, dt: es.enter_context(sbt(name, shape, dt))
            nb = NormBufs(es)
            xn, t_xn = nb.xn, nb.t_xn
            act = sb("act", [128, NM, TT], BF16)
            t_act = [Tok() for _ in range(NM)]
            wgu = [sb("wgu%d" % i, [128, 2, 8, 128], BF16) for i in range(3)]
            t_wgu = [Tok() for _ in range(3)]
            d_wgu = get_sems("wgu", ["wgu0", "wgu1", "wgu2"])
            d_wd = get_sems("wd", ["wd0", "wd1"])
            wd = [sb("wd%d" % i, [128, NM, 128], BF16) for i in range(2)]
            t_wd = [Tok() for _ in range(2)]
            sg = [sb("sg%d" % i, [128, TT], BF16) for i in range(2)]
            t_sg = [Tok() for _ in range(2)]
            pg, pu, py = (1, 2), (3, 4), (5, 6)
            gi = [0]
            di = [0]

            def load_gu(m):
                s = gi[0] % 3
                gi[0] += 1
                tsrc = t_wgub0[m] if (f == 0 and m < 4) else t_wgub[f]
                op("sp", lambda h: h.dma_start(out=wgu[s][:], in_=wgu_b[f * NM + m]),
                   rd=[tsrc], wr=[t_wgu[s]], dsem=d_wgu[s])
                return s

            def load_d(c):
                s = di[0] % 2
                di[0] += 1
                op("sp", lambda h: h.dma_start(out=wd[s][:], in_=wd_b[f * 8 + c]),
                   rd=[t_wdb[f]], wr=[t_wd[s]], dsem=d_wd[s])
                return s

            nb.norm(0, gbase, 0, sq_eng="act")
            for tt in range(NT):
                b = tt % 2
                sl = slice(tt * TT, (tt + 1) * TT)
                slots = {0: load_gu(0)}
                slots[1] = load_gu(1)
                dslots = {}
                for m in range(NM):
                    if m + 2 < NM:
                        slots[m + 2] = load_gu(m + 2)
                    if m == NM - 2:
                        dslots[0] = load_d(0)
                    s = slots[m]
                    par = m % 2
                    for g, bank in ((0, pg[par]), (1, pu[par])):
                        for k in range(8):
                            op("pe", lambda h: h.matmul(ps[bank][:], wgu[s][:, g, k, :], xn[b][:, k, :],
                                                        start=(k == 0), stop=(k == 7)),
                               rd=[t_wgu[s], t_xn[b]], wr=[tps[bank]], inc=(k == 7))
                    op("act", lambda h: h.activation(out=sg[par][:], in_=ps[pg[par]][:], func=AF.Silu),
                       rd=[tps[pg[par]]], wr=[t_sg[par]])
                    op("dve", lambda h: h.tensor_tensor(out=act[:, m, :], in0=sg[par][:], in1=ps[pu[par]][:],
                                                        op=ALU.mult),
                       rd=[t_sg[par], tps[pu[par]]], wr=[t_act[m]])
                    if tt + 1 < NT:
                        if 1 <= m <= 8:
                            nb.norm_stage(tt + 1, gbase, 1 - b, m - 1)
                        elif m == 9:
                            nb.norm_stage(tt + 1, gbase, 1 - b, 8)
                            nb.norm_stage(tt + 1, gbase, 1 - b, 9)
                        elif 10 <= m <= 17:
                            nb.norm_stage(tt + 1, gbase, 1 - b, m)
                    if m % 8 == 1:
                        bgconv(1)
                for c in range(8):
                    if c + 1 < 8:
                        dslots[c + 1] = load_d(c + 1)
                    s = dslots[c]
                    bank = py[c % 2]
                    for k in range(NM):
                        op("pe", lambda h: h.matmul(ps[bank][:], wd[s][:, k, :], act[:, k, :],
                                                    start=(k == 0), stop=(k == NM - 1)),
                           rd=[t_wd[s], t_act[k]], wr=[tps[bank]], inc=(k == NM - 1))
                    op("dve", lambda h: h.scalar_tensor_tensor(out=xs[:, c, sl], in0=ps[bank][:], scalar=0.5,
                                                               in1=xs[:, c, sl], op0=ALU.mult, op1=ALU.add),
                       rd=[tps[bank], tx[c][tt]], wr=[tx[c][tt]])
            kb.barrier()

    hn_i = [0]
    pr_i = [0]

    def head_norm_a(bank, tmp):
        sqh_l, t_sqh_l, rq_l, t_rq_l, ssb = tmp
        j = hn_i[0] % len(sqh_l)
        hn_i[0] += 1
        sqh, t_sqh = sqh_l[j], t_sqh_l[j]
        op("act", lambda h: h.activation(out=sqh[:], in_=ps[bank][:], func=AF.Square),
           rd=[tps[bank]], wr=[t_sqh])
        op("pe", lambda h: h.matmul(ps[ssb][:], ones_blk, sqh[:], start=True, stop=True),
           rd=[t_sqh, t_const], wr=[tps[ssb]])
        return (bank, ssb, rq_l[j], t_rq_l[j])

    def head_norm_b(st, gcol, dst_ap, t_dst):
        bank, ssb, rq, t_rq = st
        op("act", lambda h: h.activation(out=rq[:], in_=ps[ssb][:], func=AF.Ln, scale=1.0 / 64, bias=EPS),
           rd=[tps[ssb]], wr=[t_rq])
        op("act", lambda h: h.activation(out=rq[:], in_=rq[:], func=AF.Exp, scale=-0.5), rd=[t_rq], wr=[t_rq])
        op("dve", lambda h: h.scalar_tensor_tensor(out=dst_ap, in0=ps[bank][:], scalar=pcols[:, gcol:gcol + 1],
                                                   in1=rq[:], op0=ALU.mult, op1=ALU.mult),
           rd=[tps[bank], t_rq, t_const], wr=[t_dst])

    rot = {"s": 0, "p": 0, "e": 0}
    attn_q = []
    SB_ = (0, 1, 2, 6)
    PB_ = (3, 4, 5, 7)
    NE_ = 5

    def attn_class(qpad, t_q, k2, t_k, V, t_V, vb0, r, d, mcol, E, t_E, fin, n_lo, n_hi):
        mcat = cbf[:, mcol:mcol + 256]
        queue = attn_q

        def scores(n):
            a = SB_[rot["s"] % 4]
            rot["s"] += 1
            e = rot["e"] % NE_
            rot["e"] += 1
            qc = cols(qpad, slice(0, 128), r + d * 128 * n, 128, d)
            kd = cols(k2, slice(0, 128), r + d * 128 * n, 128, d)
            if n > 0:
                kp = cols(k2, slice(0, 128), r + d * 128 * (n - 1), 128, d)
                op("pe", lambda h: h.matmul(ps[a][:, 0:256], ident, mcat, start=True, stop=False),
                   rd=[t_const], wr=[tps[a]], inc=False)
                op("pe", lambda h: h.matmul(ps[a][:, 0:128], kp, qc, start=False, stop=False),
                   rd=[t_k, t_q], wr=[tps[a]], inc=False)
                op("pe", lambda h: h.matmul(ps[a][:, 128:256], kd, qc, start=False, stop=True),
                   rd=[t_k, t_q], wr=[tps[a]])
                op("act", lambda h: h.activation(out=E[e][:, 0:256], in_=ps[a][:, 0:256], func=AF.Exp, scale=0.125),
                   rd=[tps[a]], wr=[t_E[e]])
            else:
                op("pe", lambda h: h.matmul(ps[a][:, 128:256], ident, mcat[:, 128:256], start=True, stop=False),
                   rd=[t_const], wr=[tps[a]], inc=False)
                op("pe", lambda h: h.matmul(ps[a][:, 128:256], kd, qc, start=False, stop=True),
                   rd=[t_k, t_q], wr=[tps[a]])
                op("act", lambda h: h.activation(out=E[e][:, 128:256], in_=ps[a][:, 128:256], func=AF.Exp,
                                                 scale=0.125),
                   rd=[tps[a]], wr=[t_E[e]])
            return e

        def pv(n, e):
            bnk = PB_[rot["p"] % 4]
            rot["p"] += 1
            if n > 0:
                op("pe", lambda h: h.matmul(ps[bnk][:, 0:128], V[:, vb0 + n - 1, :], E[e][:, 0:128],
                                            start=True, stop=False),
                   rd=[t_V, t_E[e]], wr=[tps[bnk]], inc=False)
            op("pe", lambda h: h.matmul(ps[bnk][:, 0:128], V[:, vb0 + n, :], E[e][:, 128:256],
                                        start=(n == 0), stop=True),
               rd=[t_V, t_E[e]], wr=[tps[bnk]], inc=False)
            if n > 0:
                op("pe", lambda h: h.matmul(ps[bnk][:, 128:256], ones_b[:], E[e][:, 0:128], start=True, stop=False),
                   rd=[t_const, t_E[e]], wr=[tps[bnk]], inc=False)
            op("pe", lambda h: h.matmul(ps[bnk][:, 128:256], ones_b[:], E[e][:, 128:256], start=(n == 0), stop=True),
               rd=[t_const, t_E[e]], wr=[tps[bnk]])
            fin(n, bnk)

        for n in range(n_lo, n_hi):
            e = scores(n)
            queue.append(lambda n=n, e=e: pv(n, e))
            if len(queue) > 3:
                queue.pop(0)()

    def attn_flush():
        while attn_q:
            attn_q.pop(0)()

    def out_proj(layer):
        b0 = mix_base(layer) + (14 if layer % 2 == 0 else 16)
        with ExitStack() as es:
            sb = lambda name, shape, dt: es.enter_context(sbt(name, shape, dt))
            wout = sb("wout", [128, 8, 1024], BF16)
            t_wout = Tok()
            mixt = [sb("mixt%d" % i, [128, 8, TT], BF16) for i in range(2)]
            t_mixt = [Tok() for _ in range(2)]
            d_wout, d_m0, d_m1 = get_sems("oproj", ["wout", "mixt0", "mixt1"])
            d_mixt = [d_m0, d_m1]
            op("sp", lambda h: h.dma_start(out=wout[:], in_=wmix_b[b0:b0 + 8].rearrange("u p n -> p u n")),
               rd=[t_wmixb[layer]], wr=[t_wout], dsem=d_wout)

            def load(tt):
                b = tt % 2
                op("sp", lambda h: h.dma_start(out=mixt[b][:],
                                               in_=mixT_d[:, tt * TT:(tt + 1) * TT].rearrange("(k p) t -> p k t", p=128)),
                   rd=[t_mixd], wr=[t_mixt[b]], dsem=d_mixt[b])
            load(0)
            for tt in range(NT):
                b = tt % 2
                sl = slice(tt * TT, (tt + 1) * TT)
                if tt + 1 < NT:
                    load(tt + 1)
                for c in range(8):
                    bank = 5 + c % 2
                    proj8(bank, lambda k: wout[:, c, k * 128:(k + 1) * 128], mixt[b], t_wout, t_mixt[b])
                    if c == 0:
                        bgconv(1)
                    op("dve", lambda h: h.tensor_tensor(out=xs[:, c, sl], in0=ps[bank][:], in1=xs[:, c, sl], op=ALU.add),
                       rd=[tps[bank], tx[c][tt]], wr=[tx[c][tt]])
            kb.barrier()

    def even_mixer(layer):
        i = layer // 2
        gbase = PC_G + (layer * 3 + 1) * 8
        pc = PC_EV + i * EV_N
        b0 = mix_base(layer)
        conv_need(("m", layer))
        with ExitStack() as es:
            sb = lambda name, shape, dt: es.enter_context(sbt(name, shape, dt))
            nb = NormBufs(es, nsq=8, nxn=3)
            win = sb("win", [128, 8, 1024], BF16)
            t_win = Tok()
            sig = [sb("sig%d" % j, [128, TT], BF16) for j in range(4)]
            t_sig = [Tok() for _ in range(4)]
            ust = [sb("ust%d" % j, [128, 4, TT], BF16) for j in range(2)]
            t_ust = [Tok() for _ in range(2)]
            d_win, d_u0, d_u1 = get_sems("e1a", ["e1aw", "e1au0", "e1au1"])
            d_u = [d_u0, d_u1]
            op("sp", lambda h: h.dma_start(out=win[:], in_=wmix_b[b0:b0 + 8].rearrange("u p n -> p u n")),
               rd=[t_wmixb[layer]], wr=[t_win], dsem=d_win)
            nb.norm(0, gbase, 0, sq_eng="act")
            nb.norm(1, gbase, 1)
            for tt in range(NT):
                b = tt % 2
                sl = slice(tt * TT, (tt + 1) * TT)
                xn, t_xn = nb.xn[tt % 3], nb.t_xn[tt % 3]
                bgconv(1)
                for c in range(4):
                    bv, bg = ((1, 2), (3, 4), (5, 6))[pr_i[0] % 3]
                    pr_i[0] += 1
                    proj8(bv, lambda k: win[:, c, k * 128:(k + 1) * 128], xn, t_win, t_xn)
                    proj8(bg, lambda k: win[:, 4 + c, k * 128:(k + 1) * 128], xn, t_win, t_xn)
                    op("act", lambda h: h.activation(out=sig[c][:], in_=ps[bg][:], func=AF.Sigmoid),
                       rd=[tps[bg]], wr=[t_sig[c]])
                    op("dve", lambda h: h.tensor_tensor(out=ust[b][:, c, :], in0=sig[c][:], in1=ps[bv][:],
                                                        op=ALU.mult),
                       rd=[t_sig[c], tps[bv]], wr=[t_ust[b]])
                    if tt + 2 < NT and c in (0, 3):
                        if c == 0:
                            nb.norm_part(tt + 2, gbase, (tt + 2) % 3, 0)
                        else:
                            for part in (1, 2, 3):
                                nb.norm_part(tt + 2, gbase, (tt + 2) % 3, part)
                op("pool", lambda h: h.dma_start(out=u_d[:, sl].rearrange("(c p) t -> p c t", p=128), in_=ust[b][:]),
                   rd=[t_ust[b]], wr=[t_ud], dsem=d_u[b])
            kb.barrier()
        with ExitStack() as es:
            sb = lambda name, shape, dt: es.enter_context(sbt(name, shape, dt))
            diag = sb("diag", [128, 4, CONVW, 128], BF16)
            t_diag = [Tok() for _ in range(4)]
            ubuf = [sb("ubuf%d" % j, [128, 4, 30 + TT], BF16) for j in range(2)]
            t_ubuf = [Tok() for _ in range(2)]
            u2 = sb("u2", [128, 4, TT], F32)
            t_u2 = [Tok() for _ in range(4)]
            u2b = [sb("u2b%d" % j, [128, TT], BF16) for j in range(2)]
            t_u2b = [Tok() for _ in range(2)]
            u2q = [sb("u2q%d" % j, [128, TT], BF16) for j in range(2)]
            t_u2q = [Tok() for _ in range(2)]
            mean = sb("mean", [128, TT], F32)
            t_mean = Tok()
            msq = sb("msq", [128, TT], F32)
            t_msq = Tok()
            yb = [sb("yb%d" % j, [128, 4, TT], BF16) for j in range(2)]
            t_yb = [Tok() for _ in range(2)]
            d_ub0, d_ub1, d_y0, d_y1 = get_sems("e1b", ["e1bu0", "e1bu1", "e1by0", "e1by1"])
            d_ub = [d_ub0, d_ub1]
            d_yb = [d_y0, d_y1]
            for c in range(4):
                for j in range(CONVW):
                    e = "dve"
                    col = pc + EV_CW + c * CONVW + j
                    op(e, lambda h: h.tensor_scalar(out=diag[:, c, j, :], in0=ident, scalar1=pcols[:, col:col + 1],
                                                    scalar2=None, op0=ALU.mult),
                       rd=[t_const], wr=[t_diag[c]])
            op("pool", lambda h: h.memset(ubuf[0][:, :, 0:30], 0.0), wr=[t_ubuf[0]])

            def load_u(tt):
                b = tt % 2
                lo = tt * TT - 30
                if tt == 0:
                    op("sp", lambda h: h.dma_start(out=ubuf[b][:, :, 30:30 + TT],
                                                   in_=u_d[:, 0:TT].rearrange("(c p) t -> p c t", p=128)),
                       rd=[t_ud], wr=[t_ubuf[b]], dsem=d_ub[b])
                else:
                    op("sp", lambda h: h.dma_start(out=ubuf[b][:],
                                                   in_=u_d[:, lo:lo + 30 + TT].rearrange("(c p) t -> p c t", p=128)),
                       rd=[t_ud], wr=[t_ubuf[b]], dsem=d_ub[b])
            load_u(0)
            for tt in range(NT):
                b = tt % 2
                sl = slice(tt * TT, (tt + 1) * TT)
                if tt + 1 < NT:
                    load_u(tt + 1)
                bgconv(1)
                for c in range(4):
                    bank = 5 + c % 2
                    for j in range(CONVW):
                        op("pe", lambda h: h.matmul(ps[bank][:], diag[:, c, j, :], ubuf[b][:, c, j:j + TT],
                                                    start=(j == 0), stop=(j == CONVW - 1)),
                           rd=[t_diag[c], t_ubuf[b]], wr=[tps[bank]], inc=(j == CONVW - 1))
                    cb = pc + EV_CB + c
                    op("act", lambda h: h.activation(out=u2[:, c, :], in_=ps[bank][:], func=AF.Identity,
                                                     bias=pcols[:, cb:cb + 1], scale=1.0),
                       rd=[tps[bank], t_const], wr=[t_u2[c]])
                    op("dve", lambda h: h.tensor_copy(out=u2b[c % 2][:], in_=u2[:, c, :]),
                       rd=[t_u2[c]], wr=[t_u2b[c % 2]])
                    op("act", lambda h: h.activation(out=u2q[c % 2][:], in_=u2[:, c, :], func=AF.Square),
                       rd=[t_u2[c]], wr=[t_u2q[c % 2]])

                    def stats_mm(cc):
                        op("pe", lambda h: h.matmul(ps[7][:], ones_b[:], u2b[cc % 2][:], start=(cc == 0), stop=(cc == 3)),
                           rd=[t_u2b[cc % 2], t_const], wr=[tps[7]])
                        op("pe", lambda h: h.matmul(ps[0][:], ones_b[:], u2q[cc % 2][:], start=(cc == 0), stop=(cc == 3)),
                           rd=[t_u2q[cc % 2], t_const], wr=[tps[0]])
                    if c >= 1:
                        stats_mm(c - 1)
                    if c == 3:
                        stats_mm(3)
                op("act", lambda h: h.activation(out=mean[:], in_=ps[7][:], func=AF.Identity, scale=1.0 / 512),
                   rd=[tps[7]], wr=[t_mean])
                op("dve", lambda h: h.tensor_tensor(out=msq[:], in0=mean[:], in1=mean[:], op=ALU.mult),
                   rd=[t_mean], wr=[t_msq])
                op("dve", lambda h: h.scalar_tensor_tensor(out=msq[:], in0=ps[0][:], scalar=1.0 / 512, in1=msq[:],
                                                           op0=ALU.mult, op1=ALU.subtract),
                   rd=[tps[0], t_msq], wr=[t_msq])
                op("act", lambda h: h.activation(out=msq[:], in_=msq[:], func=AF.Ln, bias=EPS, scale=1.0),
                   rd=[t_msq], wr=[t_msq])
                op("act", lambda h: h.activation(out=msq[:], in_=msq[:], func=AF.Exp, scale=-0.5),
                   rd=[t_msq], wr=[t_msq])
                for c in range(4):
                    op("dve", lambda h: h.tensor_tensor(out=u2[:, c, :], in0=u2[:, c, :], in1=mean[:], op=ALU.subtract),
                       rd=[t_u2[c], t_mean], wr=[t_u2[c]])
                    op("dve", lambda h: h.tensor_tensor(out=u2[:, c, :], in0=u2[:, c, :], in1=msq[:], op=ALU.mult),
                       rd=[t_u2[c], t_msq], wr=[t_u2[c]])
                    lg, lb = pc + EV_LG + c, pc + EV_LB + c
                    op("act", lambda h: h.activation(out=yb[b][:, c, :], in_=u2[:, c, :], func=AF.Silu,
                                                     bias=pcols[:, lb:lb + 1], scale=pcols[:, lg:lg + 1]),
                       rd=[t_u2[c], t_const], wr=[t_yb[b]])
                op("pool", lambda h: h.dma_start(out=mixT_d[0:512, sl].rearrange("(c p) t -> p c t", p=128),
                                                 in_=yb[b][:]),
                   rd=[t_yb[b]], wr=[t_mixd], dsem=d_yb[b])
            kb.barrier()
        with ExitStack() as es:
            sb = lambda name, shape, dt: es.enter_context(sbt(name, shape, dt))
            nb = NormBufs(es, nsq=8, nxn=3)
            win = sb("win2", [128, 6, 1024], BF16)
            t_win = Tok()
            sqh = [sb("sqh%d" % j, [128, TT], BF16) for j in range(3)]
            rq = [sb("rq%d" % j, [128, TT], F32) for j in range(3)]
            t_sqh, t_rq = [Tok() for _ in range(3)], [Tok() for _ in range(3)]
            qst = [sb("qst%d" % j, [128, 5, TT], BF16) for j in range(2)]
            t_qst = [Tok() for _ in range(2)]
            vst = [sb("vst%d" % j, [128, 4, 128], BF16) for j in range(2)]
            t_vst = [Tok() for _ in range(2)]
            d_win, d_q0, d_q1, d_v0, d_v1 = get_sems("e2", ["e2w", "e2q0", "e2q1", "e2v0", "e2v1"])
            d_q = [d_q0, d_q1]
            d_v = [d_v0, d_v1]
            op("sp", lambda h: h.dma_start(out=win[:], in_=wmix_b[b0 + 8:b0 + 14].rearrange("u p n -> p u n")),
               rd=[t_wmixb[layer]], wr=[t_win], dsem=d_win)
            nb.norm(0, gbase, 0, sq_eng="act")
            nb.norm(1, gbase, 1)
            for tt in range(NT):
                b = tt % 2
                sl = slice(tt * TT, (tt + 1) * TT)
                xn, t_xn = nb.xn[tt % 3], nb.t_xn[tt % 3]
                bgconv(1)
                pend = None
                for c in range(5):
                    bank = (1, 2, 6)[pr_i[0] % 3]
                    ssb = (3, 4, 7)[pr_i[0] % 3]
                    pr_i[0] += 1
                    proj8(bank, lambda k: win[:, c, k * 128:(k + 1) * 128], xn, t_win, t_xn)
                    st = head_norm_a(bank, (sqh, t_sqh, rq, t_rq, ssb))
                    if pend is not None:
                        head_norm_b(*pend)
                    pend = (st, pc + (EV_QG if c < 4 else EV_KG), qst[b][:, c, :], t_qst[b])
                    if tt + 2 < NT and c in (0, 4):
                        if c == 0:
                            nb.norm_part(tt + 2, gbase, (tt + 2) % 3, 0)
                        else:
                            for part in (1, 2, 3):
                                nb.norm_part(tt + 2, gbase, (tt + 2) % 3, part)
                head_norm_b(*pend)
                for tb in range(4):
                    for k in range(8):
                        op("pe", lambda h: h.matmul(ps[5][:, tb * 128:(tb + 1) * 128], xn[:, k, tb * 128:(tb + 1) * 128],
                                                    win[:, 5, k * 128:(k + 1) * 128], start=(k == 0), stop=(k == 7)),
                           rd=[t_win, t_xn], wr=[tps[5]], inc=(k == 7))
                op("act", lambda h: h.copy(out=vst[b][:], in_=ps[5][:].rearrange("p (a f) -> p a f", a=4)),
                   rd=[tps[5]], wr=[t_vst[b]])
                op("pool", lambda h: h.dma_start(out=qT_d[:, sl].rearrange("(c p) t -> p c t", p=128),
                                                 in_=qst[b][:, 0:4, :]),
                   rd=[t_qst[b]], wr=[t_qd], dsem=d_q[b])
                op("pool", lambda h: h.dma_start(out=kT_d[0:128, sl], in_=qst[b][:, 4, :]),
                   rd=[t_qst[b]], wr=[t_kd], dsem=d_q[b])
                op("pool", lambda h: h.dma_start(out=v_d[sl, 0:128].rearrange("(a p) f -> p a f", p=128),
                                                 in_=vst[b][:]),
                   rd=[t_vst[b]], wr=[t_vd], dsem=d_v[b])
            kb.barrier()
        with ExitStack() as es:
            sb = lambda name, shape, dt: es.enter_context(sbt(name, shape, dt))
            k2 = sb("k2", [128, S], BF16)
            V = sb("V", [128, 32, 128], BF16)
            qp = [sb("qpA", [128, S], BF16), sb("qpB", [128, S], BF16)]
            att = sb("att", [128, S], BF16)
            E = [sb("E%d" % j, [128, 256], BF16) for j in range(5)]
            HSe = S // 2
            acce = sb("acce", [128, 2, HSe], F32)
            t_acce = Tok(True)
            es_t = sb("es", [128, 8], F32)
            t_k2, t_V, t_att, t_es = Tok(), Tok(), Tok(True), Tok()
            t_qp = [Tok(), Tok()]
            t_E = [Tok() for _ in range(5)]
            d_k, d_v2, d_qa, d_qb, d_att = get_sems("eB", ["eBk", "eBv", "eBqa", "eBqb", "eBatt"])
            d_qp = [d_qa, d_qb]
            op("sp", lambda h: h.dma_start(out=k2[:], in_=kT_d[0:128, :]), rd=[t_kd], wr=[t_k2], dsem=d_k)
            vsrc = v_d[:, 0:128].rearrange("(n p) f -> p n f", p=128)
            for q4 in range(4):
                op("sp", lambda h: h.dma_start(out=V[:, q4 * 8:(q4 + 1) * 8, :], in_=vsrc[:, q4 * 8:(q4 + 1) * 8, :]),
                   rd=[t_vd], wr=[t_V], dsem=d_v2)
            op("pool", lambda h: h.memset(qp[0][64:128, :], 0.0), wr=[t_qp[0]])
            op("pool", lambda h: h.memset(qp[1][0:64, :], 0.0), wr=[t_qp[1]])
            op("act", lambda h: h.activation(out=es_t[:], in_=pcols[:, pc + EV_SK:pc + EV_SK + 8], func=AF.Exp),
               rd=[t_const], wr=[t_es])
            for hh in range(8):
                kv = hh // 4
                rows = slice(kv * 64, kv * 64 + 64)
                op("sp", lambda h: h.dma_start(out=qp[kv][rows, :], in_=qT_d[hh * 64:(hh + 1) * 64, :]),
                   rd=[t_qd], wr=[t_qp[kv]], dsem=d_qp[kv])

                for half in range(2):
                    def fin(n, bnk, half=half, rows=rows):
                        c0 = n * 128 - half * HSe
                        op("dve", lambda h: h.tensor_copy(out=acce[rows, :, c0:c0 + 128],
                                                          in_=ps[bnk][rows, 0:256].rearrange("p (a q) -> p a q", a=2)),
                           rd=[tps[bnk]], wr=[t_acce])
                    if half == 0:
                        bgconv(1)
                    attn_class(qp[kv], t_qp[kv], k2, t_k2, V, t_V, 0, 0, 1, C_MPS, E, t_E, fin, half * 16, (half + 1) * 16)
                    attn_flush()
                    op("act", lambda h: h.activation(out=acce[rows, 1, :], in_=acce[rows, 1, :], func=AF.Ln,
                                                     bias=es_t[rows, hh:hh + 1], scale=1.0),
                       rd=[t_acce, t_es], wr=[t_acce])
                    op("act", lambda h: h.activation(out=acce[rows, 1, :], in_=acce[rows, 1, :], func=AF.Exp, scale=-1.0),
                       rd=[t_acce], wr=[t_acce])
                    op("dve", lambda h: h.tensor_tensor(out=att[rows, half * HSe:(half + 1) * HSe], in0=acce[rows, 0, :],
                                                        in1=acce[rows, 1, :], op=ALU.mult),
                       rd=[t_acce], wr=[t_att])
                op("pool", lambda h: h.dma_start(out=mixT_d[512 + hh * 64:512 + (hh + 1) * 64, :], in_=att[rows, :]),
                   rd=[t_att], wr=[t_mixd], dsem=d_att)
            kb.barrier()
        out_proj(layer)

    def odd_mixer(layer):
        i = layer // 2
        gbase = PC_G + (layer * 3 + 1) * 8
        pc = PC_OD + i * OD_N
        b0 = mix_base(layer)
        conv_need(("m", layer))
        with ExitStack() as es:
            sb = lambda name, shape, dt: es.enter_context(sbt(name, shape, dt))
            nb = NormBufs(es, nsq=8, nxn=3)
            win = sb("winA", [128, 8, 1024], BF16)
            t_win = Tok()
            sqh = [sb("sqh%d" % j, [128, TT], BF16) for j in range(3)]
            rq = [sb("rq%d" % j, [128, TT], F32) for j in range(3)]
            t_sqh, t_rq = [Tok() for _ in range(3)], [Tok() for _ in range(3)]
            qst = [sb("qst%d" % j, [128, 8, TT], BF16) for j in range(2)]
            t_qst = [Tok() for _ in range(2)]
            d_win, d_q0, d_q1 = get_sems("oA1", ["oA1w", "oA1q0", "oA1q1"])
            d_q = [d_q0, d_q1]
            op("sp", lambda h: h.dma_start(out=win[:], in_=wmix_b[b0:b0 + 8].rearrange("u p n -> p u n")),
               rd=[t_wmixb[layer]], wr=[t_win], dsem=d_win)
            nb.norm(0, gbase, 0, sq_eng="act")
            nb.norm(1, gbase, 1)
            for tt in range(NT):
                b = tt % 2
                sl = slice(tt * TT, (tt + 1) * TT)
                xn, t_xn = nb.xn[tt % 3], nb.t_xn[tt % 3]
                bgconv(1)
                pend = None
                for c in range(8):
                    bank = (1, 2, 5, 6)[pr_i[0] % 4]
                    ssb = (3, 4, 7)[pr_i[0] % 3]
                    pr_i[0] += 1
                    proj8(bank, lambda k: win[:, c, k * 128:(k + 1) * 128], xn, t_win, t_xn)
                    st = head_norm_a(bank, (sqh, t_sqh, rq, t_rq, ssb))
                    if pend is not None:
                        head_norm_b(*pend)
                    pend = (st, pc + (OD_QG if c < 4 else OD_KG), qst[b][:, c, :], t_qst[b])
                    if tt + 2 < NT and c in (0, 7):
                        if c == 0:
                            nb.norm_part(tt + 2, gbase, (tt + 2) % 3, 0)
                        else:
                            for part in (1, 2, 3):
                                nb.norm_part(tt + 2, gbase, (tt + 2) % 3, part)
                head_norm_b(*pend)
                op("pool", lambda h: h.dma_start(out=qT_d[:, sl].rearrange("(c p) t -> p c t", p=128),
                                                 in_=qst[b][:, 0:4, :]),
                   rd=[t_qst[b]], wr=[t_qd], dsem=d_q[b])
                op("pool", lambda h: h.dma_start(out=kT_d[:, sl].rearrange("(c p) t -> p c t", p=128),
                                                 in_=qst[b][:, 4:8, :]),
                   rd=[t_qst[b]], wr=[t_kd], dsem=d_q[b])
            kb.barrier()
        with ExitStack() as es:
            sb = lambda name, shape, dt: es.enter_context(sbt(name, shape, dt))
            nb = NormBufs(es)
            win = sb("winB", [128, 8, 1024], BF16)
            pw = sb("pw", [128, 4, 128], BF16)
            t_win = Tok()
            vst = [sb("vst%d" % j, [128, 4, 512], BF16) for j in range(2)]
            t_vst = [Tok() for _ in range(2)]
            ubuf = sb("ubufo", [128, 4, 16 + TT], F32)
            t_ub = [Tok() for _ in range(4)]
            tAB = [[sb("tA%d" % j, [128, 16 + TT], F32), sb("tB%d" % j, [128, 16 + TT], F32)] for j in range(2)]
            t_tAB = [[Tok(), Tok()] for j in range(2)]
            pb = [sb("pb%d" % j, [128, TT], BF16) for j in range(4)]
            t_pb = [Tok() for _ in range(4)]
            pst = [sb("pst%d" % j, [128, 4, TT], BF16) for j in range(2)]
            t_pst = [Tok() for _ in range(2)]
            d_win, d_v0, d_v1, d_p0, d_p1 = get_sems("oA2", ["oA2w", "oA2v0", "oA2v1", "oA2p0", "oA2p1"])
            d_v = [d_v0, d_v1]
            d_p = [d_p0, d_p1]
            op("sp", lambda h: h.dma_start(out=win[:], in_=wmix_b[b0 + 8:b0 + 16].rearrange("u p n -> p u n")),
               rd=[t_wmixb[layer]], wr=[t_win], dsem=d_win)
            op("sp", lambda h: h.dma_start(out=pw[:].rearrange("p g c -> p (g c)"), in_=wmix_b[b0 + 24][:, 0:512]),
               rd=[t_wmixb[layer]], wr=[t_win], dsem=d_win)
            op("pool", lambda h: h.memset(ubuf[:, :, 0:16], 0.0), wr=t_ub)
            for j in range(2):
                for q in range(2):
                    op("pool", lambda h: h.memset(tAB[j][q][:], 0.0), wr=[t_tAB[j][q]])
            nb.norm(0, gbase, 0)
            W = 16 + TT

            def pool_mm(ttp):
                bp = ttp % 2
                for g in range(4):
                    op("pe", lambda h: h.matmul(ps[7][:], pw[:, g, :], pb[g][:], start=True, stop=True),
                       rd=[t_win, t_pb[g]], wr=[tps[7]])
                    sc = pc + OD_PS + g
                    op("act", lambda h: h.activation(out=pst[bp][:, g, :], in_=ps[7][:], func=AF.Identity,
                                                     scale=pcols[:, sc:sc + 1]),
                       rd=[tps[7], t_const], wr=[t_pst[bp]])
                op("pool", lambda h: h.dma_start(
                    out=mixT_d[512:1024, ttp * TT:(ttp + 1) * TT].rearrange("(c p) t -> p c t", p=128), in_=pst[bp][:]),
                   rd=[t_pst[bp]], wr=[t_mixd], dsem=d_p[bp])

            for tt in range(NT):
                b = tt % 2
                sl = slice(tt * TT, (tt + 1) * TT)
                xn, t_xn = nb.xn[b], nb.t_xn[b]
                bgconv(1)
                for tb in range(4):
                    bank = 5 + tb % 2
                    for k in range(8):
                        op("pe", lambda h: h.matmul(ps[bank][:], xn[:, k, tb * 128:(tb + 1) * 128],
                                                    win[:, k // 2, (k % 2) * 512:(k % 2) * 512 + 512],
                                                    start=(k == 0), stop=(k == 7)),
                           rd=[t_win, t_xn], wr=[tps[bank]], inc=(k == 7))
                    if tb % 2 == 0:
                        op("act", lambda h: h.copy(out=vst[b][:, tb, :], in_=ps[bank][:]),
                           rd=[tps[bank]], wr=[t_vst[b]])
                    else:
                        op("dve", lambda h: h.tensor_copy(out=vst[b][:, tb, :], in_=ps[bank][:]),
                           rd=[tps[bank]], wr=[t_vst[b]])
                    if tb == 0 and tt + 1 < NT:
                        nb.norm(tt + 1, gbase, 1 - b, sq_eng="act")
                op("pool", lambda h: h.dma_start(out=v_d[sl, :].rearrange("(a p) f -> p a f", p=128), in_=vst[b][:]),
                   rd=[t_vst[b]], wr=[t_vd], dsem=d_v[b])
                for g in range(4):
                    bank = 1 + g % 2
                    proj8(bank, lambda k: win[:, 4 + g, k * 128:(k + 1) * 128], xn, t_win, t_xn)
                    op("act", lambda h: h.copy(out=ubuf[:, g, 16:W], in_=ps[bank][:]),
                       rd=[tps[bank]], wr=[t_ub[g]])
                if tt > 0:
                    pool_mm(tt - 1)
                for g in range(4):
                    w = POOL_SIZES[g]
                    ug = ubuf[:, g, :]
                    src, t_src = ug, t_ub[g]
                    bufs = [(tAB[g % 2][0], t_tAB[g % 2][0]), (tAB[g % 2][1], t_tAB[g % 2][1])]
                    sh = 1
                    step = 0
                    while sh < w:
                        dst, t_dst = bufs[step % 2]
                        op("dve", lambda h: h.tensor_tensor(out=dst[:, sh:W], in0=src[:, sh:W], in1=src[:, 0:W - sh],
                                                            op=ALU.add),
                           rd=[t_src], wr=[t_dst])
                        src, t_src = dst, t_dst
                        sh *= 2
                        step += 1
                    j = g
                    op("dve", lambda h: h.scalar_tensor_tensor(out=pb[j][:], in0=src[:, 16:W], scalar=1.0 / w,
                                                               in1=ug[:, 16:W], op0=ALU.mult, op1=ALU.subtract),
                       rd=[t_src, t_ub[g]], wr=[t_pb[j]])
                    if tt == 0:
                        ic = icnt[:, g * 16:(g + 1) * 16]
                        op("dve", lambda h: h.tensor_tensor(out=src[:, 16:32], in0=src[:, 16:32], in1=ic, op=ALU.mult),
                           rd=[t_src, t_const], wr=[t_src])
                        op("dve", lambda h: h.tensor_tensor(out=pb[j][:, 0:16], in0=src[:, 16:32], in1=ug[:, 16:32],
                                                            op=ALU.subtract),
                           rd=[t_src, t_ub[g]], wr=[t_pb[j]])
                    op("pool", lambda h: h.tensor_copy(out=ubuf[:, g, 0:16], in_=ubuf[:, g, TT:TT + 16]),
                       rd=[t_ub[g]], wr=[t_ub[g]])
            pool_mm(NT - 1)
            kb.barrier()
        with ExitStack() as es:
            sb = lambda name, shape, dt: es.enter_context(sbt(name, shape, dt))
            k2b = [sb("k2_%d" % j, [128, S], BF16) for j in range(2)]
            t_k2b = [Tok(), Tok()]
            Vb = [sb("V%d" % j, [128, 32, 128], BF16) for j in range(2)]
            qp = [sb("qpA", [128, S], BF16), sb("qpB", [128, S], BF16)]
            HS = S // 2
            acc = sb("acc", [128, 2, HS], F32)
            att = sb("atto", [128, HS], BF16)
            E = [sb("E%d" % j, [128, 256], BF16) for j in range(5)]
            t_att = Tok()
            t_accb = [[Tok(True) for _ in range(3)] for _ in range(2)]
            t_Vb = [Tok(), Tok()]
            t_qp = [Tok(), Tok()]
            t_E = [Tok() for _ in range(5)]
            d_k, d_va, d_vb, d_qa, d_qb, d_att = get_sems("oB", ["oBk", "oBva", "oBvb", "oBqa", "oBqb", "oBatt"])
            d_V = [d_va, d_vb]
            d_qp = [d_qa, d_qb]
            op("pool", lambda h: h.memset(qp[0][64:128, :], 0.0), wr=[t_qp[0]])
            op("pool", lambda h: h.memset(qp[1][0:64, :], 0.0), wr=[t_qp[1]])
            vi = [0]

            def load_V(c, d):
                j = vi[0] % 2
                vi[0] += 1
                src = v_d[:, c * 128:(c + 1) * 128].rearrange("(n p r) f -> p r n f", p=128, r=d)
                nb_ = 32 // d
                for r in range(d):
                    step = min(nb_, 8)
                    for n0 in range(0, nb_, step):
                        op("sp", lambda h: h.dma_start(out=Vb[j][:, r * nb_ + n0:r * nb_ + n0 + step, :],
                                                       in_=src[:, r, n0:n0 + step, :]),
                           rd=[t_vd], wr=[t_Vb[j]], dsem=d_V[j])
                return j

            d_k2 = get_sems("oBk2", ["oBk2"])[0]

            def load_k2(cc):
                op("sp", lambda h: h.dma_start(out=k2b[cc % 2][:], in_=kT_d[cc * 128:(cc + 1) * 128, :]),
                   rd=[t_kd], wr=[t_k2b[cc % 2]], dsem=(d_k if cc % 2 == 0 else d_k2))
            load_k2(0)
            for c in range(4):
                k2, t_k2 = k2b[c % 2], t_k2b[c % 2]
                if c + 1 < 4:
                    load_k2(c + 1)
                for hp in range(2):
                    hh = 2 * c + hp
                    rows = slice(hp * 64, hp * 64 + 64)
                    op("sp", lambda h: h.dma_start(out=qp[hp][rows, :], in_=qT_d[hh * 64:(hh + 1) * 64, :]),
                       rd=[t_qd], wr=[t_qp[hp]], dsem=d_qp[hp])
                for half in range(2):
                    for bi, (wnd, d) in enumerate(DIL):
                        vj = load_V(c, d)
                        bgconv(1)
                        nblk = 32 // d
                        per_half = nblk // 2
                        for hp in range(2):
                            rows = slice(hp * 64, hp * 64 + 64)
                            for r in range(d):
                                def fin(n, bnk, r=r, d=d, half=half, bi=bi, hp=hp, rows=rows):
                                    c0 = r + d * 128 * n - half * HS
                                    dst = cols3(acc, rows, c0, 128, d)
                                    src = ps[bnk][rows, 0:256].rearrange("p (a q) -> p a q", a=2)
                                    if bi == 0:
                                        op("dve", lambda h: h.tensor_copy(out=dst, in_=src),
                                           rd=[tps[bnk]], wr=[t_accb[hp][0]])
                                    else:
                                        op("dve", lambda h: h.tensor_tensor(out=dst, in0=src, in1=dst, op=ALU.add),
                                           rd=[tps[bnk], t_accb[hp][bi - 1]], wr=[t_accb[hp][bi]])
                                attn_class(qp[hp], t_qp[hp], k2, t_k2, Vb[vj], t_Vb[vj], r * nblk, r, d, C_MPD,
                                           E, t_E, fin, half * per_half, (half + 1) * per_half)
                    attn_flush()
                    for hp in range(2):
                        rows = slice(hp * 64, hp * 64 + 64)
                        op("act", lambda h: h.activation(out=acc[rows, 1, :], in_=acc[rows, 1, :], func=AF.Ln),
                           rd=t_accb[hp], wr=[t_accb[hp][2]])
                        op("act", lambda h: h.activation(out=acc[rows, 1, :], in_=acc[rows, 1, :], func=AF.Exp, scale=-1.0),
                           rd=t_accb[hp], wr=[t_accb[hp][2]])
                        op("dve", lambda h: h.tensor_tensor(out=att[rows, :], in0=acc[rows, 0, :], in1=acc[rows, 1, :],
                                                            op=ALU.mult),
                           rd=t_accb[hp], wr=[t_att])
                    op("pool", lambda h: h.dma_start(out=mixT_d[c * 128:(c + 1) * 128, half * HS:(half + 1) * HS],
                                                     in_=att[:]),
                       rd=[t_att], wr=[t_mixd], dsem=d_att)
            kb.barrier()
        out_proj(layer)

    phases = []
    for layer in range(DEPTH):
        phases.append((layer, "ffn1"))
        phases.append((layer, "mix"))
        phases.append((layer, "ffn2"))
    if stop_after is not None:
        phases = phases[:phases.index(stop_after) + 1]
    phases = [p for p in phases if p not in skip]
    need_ffn = sorted({l * 2 + (0 if p == "ffn1" else 1) for l, p in phases if p != "mix"})
    need_mix = sorted({l for l, p in phases if p == "mix"})
    conv_plan(phases)
    for layer, p in phases:
        if p == "ffn1":
            ffn(layer * 2, layer, 0)
        elif p == "ffn2":
            ffn(layer * 2 + 1, layer, 1)
        elif layer % 2 == 0:
            even_mixer(layer)
        else:
            odd_mixer(layer)

    ds_o = kb.dsem("o")
    for c in range(8):
        for t in range(NT):
            op("sp", lambda h: h.dma_start(out=y_out[c * 128:(c + 1) * 128, t * TT:(t + 1) * TT],
                                           in_=xs[:, c, t * TT:(t + 1) * TT]),
               rd=[tx[c][t]], dsem=ds_o)
    nc.sync.wait_ge(ds_o.h, ds_o.n)
    print("built: ins=%d waits=%d" % (kb.nins, kb.nwait), {e: s.n for e, s in kb.prog.items()},
          "dsems=%d" % len(kb._dsems))
    return nc


def prep_inputs(norm_g, ffn_w_gate, ffn_w_up, ffn_w_down,
                ev_w_in, ev_w_out, ev_conv_w, ev_conv_b, ev_ln_g, ev_ln_b,
                ev_q_norm_g, ev_k_norm_g, ev_sinks,
                od_w_in, od_w_out, od_q_norm_g, od_k_norm_g, od_pool_w, od_pool_scale):
    f32 = np.float32
    wgu = np.zeros((DEPTH * 2, NM, 128, 2, 8, 128), f32)
    wd = np.zeros((DEPTH * 2, 8, 128, NM, 128), f32)
    for l in range(DEPTH):
        for w in range(2):
            f = l * 2 + w
            for g, src in ((0, ffn_w_gate), (1, ffn_w_up)):
                wp = np.zeros((D, DFFP), f32)
                wp[:, :DFF] = src[l, w]
                wgu[f, :, :, g] = wp.reshape(8, 128, NM, 128).transpose(2, 1, 0, 3)
            dp = np.zeros((DFFP, D), f32)
            dp[:DFF] = ffn_w_down[l, w]
            wd[f] = dp.reshape(NM, 128, 8, 128).transpose(2, 1, 0, 3)

    def img(w):
        n = w.shape[1] // 128
        return w.reshape(8, 128, n, 128).transpose(2, 1, 0, 3).reshape(n, 128, 1024)

    wmix = np.zeros((N_UNITS, 128, 1024), f32)
    for l in range(DEPTH):
        i = l // 2
        b0 = mix_base(l)
        if l % 2 == 0:
            wmix[b0:b0 + 14] = img(ev_w_in[i])
            wmix[b0 + 14:b0 + 22] = img(ev_w_out[i])
        else:
            wi = od_w_in[i]
            wmix[b0:b0 + 8] = img(wi[:, 0:1024])
            v = wi[:, 1024:1536].reshape(8, 128, 512).transpose(1, 0, 2)
            wmix[b0 + 8:b0 + 12] = v.reshape(128, 4, 1024).transpose(1, 0, 2)
            wmix[b0 + 12:b0 + 16] = img(wi[:, 1536:2048])
            wmix[b0 + 16:b0 + 24] = img(od_w_out[i])
            wmix[b0 + 24, :, 0:512] = od_pool_w[i].transpose(1, 0, 2).reshape(128, 512)
    pcols = np.zeros((128, PC_N), f32)
    pcols[:, PC_G:PC_G + 96] = norm_g.reshape(DEPTH * 3, 8, 128).transpose(2, 0, 1).reshape(128, 96)
    for i in range(2):
        pc = PC_EV + i * EV_N
        pcols[:, pc + EV_CW:pc + EV_CW + 124] = ev_conv_w[i].reshape(CONVW, 4, 128).transpose(2, 1, 0).reshape(128, 124)
        pcols[:, pc + EV_CB:pc + EV_CB + 4] = ev_conv_b[i].reshape(4, 128).T
        pcols[:, pc + EV_LG:pc + EV_LG + 4] = ev_ln_g[i].reshape(4, 128).T
        pcols[:, pc + EV_LB:pc + EV_LB + 4] = ev_ln_b[i].reshape(4, 128).T
        pcols[:, pc + EV_QG] = np.tile(ev_q_norm_g[i], 2)
        pcols[:, pc + EV_KG] = np.tile(ev_k_norm_g[i], 2)
        pcols[:, pc + EV_SK:pc + EV_SK + 8] = ev_sinks[i][None, :]
        po = PC_OD + i * OD_N
        pcols[:, po + OD_QG] = np.tile(od_q_norm_g[i], 2)
        pcols[:, po + OD_KG] = np.tile(od_k_norm_g[i], 2)
        pcols[:, po + OD_PS:po + OD_PS + 4] = od_pool_scale[i].reshape(4, 128).T
    consts = np.zeros((128, CN), f32)
    kk = np.arange(128)[:, None]
    qq = np.arange(128)[None, :]
    consts[:, C_ID:C_ID + 128] = np.eye(128, dtype=f32)
    consts[:, C_MPS:C_MPS + 128] = np.where(kk > qq, 0.0, NEG)
    consts[:, C_MD:C_MD + 128] = np.where(kk <= qq, 0.0, NEG)
    consts[:, C_MPD:C_MPD + 128] = np.where(kk >= qq, 0.0, NEG)
    consts[:, C_MD2:C_MD2 + 128] = np.where(kk <= qq, 0.0, NEG)
    consts[:, C_OB:C_OB + 128] = (kk // 64 == qq // 64).astype(f32)
    for g, w in enumerate(POOL_SIZES):
        consts[:, C_IC + g * 16:C_IC + (g + 1) * 16] = (1.0 / np.minimum(np.arange(1, 17), w)).astype(f32)[None, :]
    return dict(wgu=wgu.reshape(DEPTH * 2 * NM, 128, 2048), wd=wd.reshape(DEPTH * 2 * 8, 128, NM * 128),
                wmix=wmix, pcols=pcols, consts=consts)


def kernel(x, norm_g, ffn_w_gate, ffn_w_up, ffn_w_down,
           ev_w_in, ev_w_out, ev_conv_w, ev_conv_b, ev_ln_g, ev_ln_b,
           ev_q_norm_g, ev_k_norm_g, ev_sinks,
           od_w_in, od_w_out, od_q_norm_g, od_k_norm_g, od_pool_w, od_pool_scale,
           _n_cores=NB, _stop_after=None, _skip=(), _trace=False):
    x = np.asarray(x, np.float32)
    a = lambda v: np.asarray(v, np.float32)
    shared = prep_inputs(a(norm_g), a(ffn_w_gate), a(ffn_w_up), a(ffn_w_down),
                         a(ev_w_in), a(ev_w_out), a(ev_conv_w), a(ev_conv_b), a(ev_ln_g), a(ev_ln_b),
                         a(ev_q_norm_g), a(ev_k_norm_g), a(ev_sinks),
                         a(od_w_in), a(od_w_out), a(od_q_norm_g), a(od_k_norm_g), a(od_pool_w), a(od_pool_scale))
    nc = build_program(stop_after=_stop_after, skip=_skip)
    in_maps = []
    for b in range(_n_cores):
        m = dict(shared)
        m["xT"] = np.ascontiguousarray(x[b].T)
        in_maps.append(m)
    if _trace:
        res = run_bass_kernel_spmd(nc, in_maps, core_ids=list(range(_n_cores)), trace=True)
        print("exec_time_ns", res.exec_time_ns)
    else:
        res = run_bass_kernel_spmd(nc, in_maps, core_ids=list(range(_n_cores)))
    out = np.stack([np.ascontiguousarray(r["yT"].T) for r in res.results], axis=0)
    return out.astype(np.float32)
```
